# Optimizing a Trainium2 kernel written in Bass

```python
import math
import jax, jax.numpy as jnp
from jax import lax
import numpy as np

D_MODEL = 1024
BATCH = 1
SEQ = 16384
DEPTH = 1

CHUNK = 64

D_MIX = D_MODEL
ATTN_HEADS = 8
HEAD_DIM = 64
D_ATTN = ATTN_HEADS * HEAD_DIM
D_CONV = D_MIX - D_ATTN
CONV_WIDTH = 3
Q_BLOCK = 128
D_IN_PROJ = 3 * D_ATTN + ATTN_HEADS + 3 * D_CONV

D_FF = 2816
PLE_DIM = 256
LN_EPS = 1e-5
RMS_EPS = 1e-6
NEG_INF = -1e30

DEEPNORM_ALPHA = (2.0 * DEPTH) ** 0.25
DEEPNORM_BETA = (8.0 * DEPTH) ** -0.25

kernel_name = "hybrid_fox_shortconv_macaron_deepnorm"


def _layer_norm(x, g, b):
    xf = x.astype(jnp.float32)
    mu = jnp.mean(xf, axis=-1, keepdims=True)
    xc = xf - mu
    var = jnp.mean(xc * xc, axis=-1, keepdims=True)
    y = xc * lax.rsqrt(var + LN_EPS) * g.astype(jnp.float32) + b.astype(jnp.float32)
    return y.astype(x.dtype)


def _rms_norm(x, g):
    xf = x.astype(jnp.float32)
    ms = jnp.mean(xf * xf, axis=-1, keepdims=True)
    return (xf * lax.rsqrt(ms + RMS_EPS) * g.astype(jnp.float32)).astype(x.dtype)


def _swiglu(x, w_in, w_out):
    gu = x @ w_in
    gate, up = jnp.split(gu, 2, axis=-1)
    return (jax.nn.silu(gate) * up) @ w_out


def _forgetting_attention(q, k, v, log_f):
    b, s, h, dh = q.shape
    nb = s // Q_BLOCK
    scale = 1.0 / math.sqrt(dh)
    c = jnp.cumsum(log_f, axis=1).transpose(0, 2, 1)
    qh = q.transpose(0, 2, 1, 3)
    kh = k.transpose(0, 2, 1, 3)
    vh = v.transpose(0, 2, 1, 3)
    q_blocks = qh.reshape(b, h, nb, Q_BLOCK, dh).transpose(2, 0, 1, 3, 4)
    c_blocks = c.reshape(b, h, nb, Q_BLOCK).transpose(2, 0, 1, 3)
    key_pos = jnp.arange(s)

    def one_block(args):
        qb, cb, blk = args
        q_pos = blk * Q_BLOCK + jnp.arange(Q_BLOCK)
        logits = jnp.einsum('bhqd,bhkd->bhqk', qb, kh,
                            preferred_element_type=jnp.float32) * scale
        logits = logits + cb[..., None] - c[:, :, None, :]
        logits = jnp.where(q_pos[:, None] >= key_pos[None, :], logits, NEG_INF)
        probs = jax.nn.softmax(logits, axis=-1)
        return jnp.einsum('bhqk,bhkd->bhqd', probs.astype(vh.dtype), vh)

    out = lax.map(one_block, (q_blocks, c_blocks, jnp.arange(nb)))
    return out.transpose(1, 0, 3, 2, 4).reshape(b, s, h * dh)


def _short_gated_conv(gate_b, gate_c, h_in, conv_w):
    u = gate_c * h_in
    y = lax.conv_general_dilated(
        u, conv_w[:, None, :].astype(u.dtype), window_strides=(1,),
        padding=[(CONV_WIDTH - 1, 0)],
        dimension_numbers=('NWC', 'WIO', 'NWC'),
        feature_group_count=D_CONV)
    return gate_b * y


def _hybrid_mixer(x, w_mix_in, b_forget, conv_w, g_attn, g_conv, w_mix_out):
    b, s, _ = x.shape
    proj = x @ w_mix_in
    o = 0
    q = proj[..., o:o + D_ATTN]; o += D_ATTN
    k = proj[..., o:o + D_ATTN]; o += D_ATTN
    v = proj[..., o:o + D_ATTN]; o += D_ATTN
    f_logit = proj[..., o:o + ATTN_HEADS]; o += ATTN_HEADS
    gate_b = proj[..., o:o + D_CONV]; o += D_CONV
    gate_c = proj[..., o:o + D_CONV]; o += D_CONV
    h_in = proj[..., o:o + D_CONV]

    log_f = jax.nn.log_sigmoid((f_logit + b_forget).astype(jnp.float32))
    attn = _forgetting_attention(
        q.reshape(b, s, ATTN_HEADS, HEAD_DIM),
        k.reshape(b, s, ATTN_HEADS, HEAD_DIM),
        v.reshape(b, s, ATTN_HEADS, HEAD_DIM),
        log_f)
    conv = _short_gated_conv(gate_b, gate_c, h_in, conv_w)

    merged = jnp.concatenate([_rms_norm(attn, g_attn), _rms_norm(conv, g_conv)], axis=-1)
    return merged @ w_mix_out


def setup_inputs(seed: int = 0) -> dict:
    key = jax.random.key(seed)
    ks = jax.random.split(key, 26)
    L, D, F = DEPTH, D_MODEL, D_FF
    f32 = jnp.float32

    def nrm(k, shape, scale):
        return jax.random.normal(k, shape, f32) * scale

    def gain(k):
        return 1.0 + 0.02 * jax.random.normal(k, (L, D), f32)

    def bias(k, n):
        return 0.02 * jax.random.normal(k, (L, n), f32)

    b_forget = (jnp.linspace(1.0, 5.0, ATTN_HEADS, dtype=f32)[None, :]
                + 0.1 * jax.random.normal(ks[6], (L, ATTN_HEADS), f32))

    return {
        "x": nrm(ks[0], (BATCH, SEQ, D), 1.0),
        "p": nrm(ks[1], (DEPTH, BATCH, SEQ, PLE_DIM), 1.0),
        "ffn1_w_in": nrm(ks[2], (L, D, 2 * F), D ** -0.5),
        "ffn1_w_out": nrm(ks[3], (L, F, D), F ** -0.5 * DEEPNORM_BETA),
        "ln1_g": gain(ks[4]), "ln1_b": bias(ks[5], D),
        "w_mix_in": nrm(ks[7], (L, D, D_IN_PROJ), D ** -0.5),
        "b_forget": b_forget,
        "conv_w": nrm(ks[8], (L, CONV_WIDTH, D_CONV), CONV_WIDTH ** -0.5),
        "g_attn": 1.0 + 0.02 * jax.random.normal(ks[9], (L, D_ATTN), f32),
        "g_conv": 1.0 + 0.02 * jax.random.normal(ks[10], (L, D_CONV), f32),
        "w_mix_out": nrm(ks[11], (L, D_MIX, D), D_MIX ** -0.5 * DEEPNORM_BETA),
        "ln2_g": gain(ks[12]), "ln2_b": bias(ks[13], D),
        "ffn2_w_in": nrm(ks[14], (L, D, 2 * F), D ** -0.5),
        "ffn2_w_out": nrm(ks[15], (L, F, D), F ** -0.5 * DEEPNORM_BETA),
        "ln3_g": gain(ks[16]), "ln3_b": bias(ks[17], D),
        "w_ple": nrm(ks[18], (L, PLE_DIM, D), PLE_DIM ** -0.5 * DEEPNORM_BETA),
        "w_ple_gate": nrm(ks[19], (L, D, D), D ** -0.5),
        "b_ple_gate": bias(ks[20], D),
        "ln4_g": gain(ks[21]), "ln4_b": bias(ks[22], D),
    }


def reference(x, p, ffn1_w_in, ffn1_w_out, ln1_g, ln1_b, w_mix_in, b_forget, conv_w,
              g_attn, g_conv, w_mix_out, ln2_g, ln2_b, ffn2_w_in, ffn2_w_out, ln3_g, ln3_b,
              w_ple, w_ple_gate, b_ple_gate, ln4_g, ln4_b):
    a = DEEPNORM_ALPHA
    for i in range(DEPTH):
        x = _layer_norm(a * x + 0.5 * _swiglu(x, ffn1_w_in[i], ffn1_w_out[i]), ln1_g[i], ln1_b[i])
        mix = _hybrid_mixer(x, w_mix_in[i], b_forget[i], conv_w[i], g_attn[i], g_conv[i], w_mix_out[i])
        x = _layer_norm(a * x + mix, ln2_g[i], ln2_b[i])
        x = _layer_norm(a * x + 0.5 * _swiglu(x, ffn2_w_in[i], ffn2_w_out[i]), ln3_g[i], ln3_b[i])
        gate = jax.nn.sigmoid(x @ w_ple_gate[i] + b_ple_gate[i])
        x = _layer_norm(a * x + gate * (p[i] @ w_ple[i]), ln4_g[i], ln4_b[i])
    return x
```

```python
import contextlib
import numpy as np
import ml_dtypes
import concourse.bass as bass
import concourse.mybir as mybir
from concourse.bass_utils import run_bass_kernel_spmd

F32 = mybir.dt.float32
BF16 = mybir.dt.bfloat16
ALU = mybir.AluOpType
AF = mybir.ActivationFunctionType

NCORES = 8
SEQ = 16384
D = 1024
DFF = 2816
NJ = DFF // 128
T = SEQ // NCORES
HALF = 1024
NHALF = T // HALF
NBLK = HALF // 128
NTILE = HALF // 512
DPROJ = 3080
NHEAD = 8
PLE = 256
ALPHA = 2.0 ** 0.25
LN_EPS = 1e-5
RMS_EPS = 1e-6
EPS_LN = LN_EPS / (ALPHA * ALPHA)

SAME_ENGINE_SYNC = True


class Buf:
    def __init__(self, name):
        self.name = name
        self.w = {}
        self.r = {}

    @staticmethod
    def _add(d, tok):
        sem, val = tok
        k = id(sem)
        if k not in d or d[k][1] < val:
            d[k] = (sem, val)


class Eng:
    def __init__(self, ctx, eng, name):
        self.eng = eng
        self.name = name
        self.sem = ctx.nc.alloc_semaphore(name + "_prog")
        self.count = 0
        self.waited = {}

    def wait(self, toks):
        for sem, val in toks:
            k = id(sem)
            if sem is self.sem and not SAME_ENGINE_SYNC:
                continue
            if self.waited.get(k, 0) >= val:
                continue
            self.eng.wait_ge(sem, val)
            self.waited[k] = val


class DmaSem:
    def __init__(self, ctx, name):
        self.sem = ctx.nc.alloc_semaphore(name)
        self.count = 0


class Ctx:
    def __init__(self, nc):
        self.nc = nc
        self.PE = Eng(self, nc.tensor, "pe")
        self.ACT = Eng(self, nc.scalar, "act")
        self.DVE = Eng(self, nc.vector, "dve")
        self.POOL = Eng(self, nc.gpsimd, "pool")
        self.SP = Eng(self, nc.sync, "sp")
        self._nsem = 0

    def dsem(self, name):
        self._nsem += 1
        return DmaSem(self, f"{name}_{self._nsem}")

    def _deps(self, reads, writes, acc):
        deps = []
        for b in reads:
            deps += list(b.w.values())
        for b in writes:
            deps += list(b.r.values())
            if not acc:
                deps += list(b.w.values())
        return deps

    def _commit(self, tok, reads, writes, acc):
        for b in reads:
            Buf._add(b.r, tok)
        for b in writes:
            if acc:
                Buf._add(b.w, tok)
            else:
                b.w = {}
                Buf._add(b.w, tok)
                b.r = {}

    def op(self, E, fn, reads=(), writes=(), acc=False):
        E.wait(self._deps(reads, writes, acc))
        ins = fn()
        E.count += 1
        ins.then_inc(E.sem, 1)
        tok = (E.sem, E.count)
        self._commit(tok, reads, writes, acc)
        return tok

    def mm_group(self, mms, reads=(), writes=(), acc=False):
        E = self.PE
        E.wait(self._deps(reads, writes, acc))
        ins = None
        for fn in mms:
            ins = fn()
        E.count += 1
        ins.then_inc(E.sem, 1)
        tok = (E.sem, E.count)
        self._commit(tok, reads, writes, acc)
        return tok

    def dma(self, Q, dsem, out, in_, reads=(), writes=(), acc=False):
        Q.wait(self._deps(reads, writes, acc))
        ins = Q.eng.dma_start(out=out, in_=in_)
        dsem.count += 16
        ins.then_inc(dsem.sem, 16)
        tok = (dsem.sem, dsem.count)
        self._commit(tok, reads, writes, acc)
        return tok

    def finish(self, bufs):
        toks = []
        for b in bufs:
            toks += list(b.w.values())
        self.SP.wait(toks)


class Slot:
    def __init__(self, ctx, t, name, dma=False):
        self.t = t
        self.b = Buf(name)
        self._ctx = ctx
        self._s = None

    @property
    def s(self):
        if self._s is None:
            self._s = self._ctx.dsem(self.b.name)
        return self._s


class Res:
    _uid = [0]

    def __init__(self, nc, ctx, stack):
        self.nc, self.ctx, self.stack = nc, ctx, stack

    def _name(self, name):
        Res._uid[0] += 1
        return f"{name}_{Res._uid[0]}"

    def sb(self, name, shape, dt, dma=False):
        t = self.stack.enter_context(self.nc.sbuf_tensor(self._name(name), list(shape), dt))
        return Slot(self.ctx, t, name, dma)

    def ps(self, name, shape, dt):
        t = self.stack.enter_context(self.nc.psum_tensor(self._name(name), list(shape), dt))
        return Slot(self.ctx, t, name)


def emit_transposes(ctx, src, nch, ident, tp, dstT, dst_cols, evac_eng):
    nc = ctx.nc
    mms = []
    for k in range(nch):
        mms.append(lambda k=k: nc.tensor.transpose(
            out=tp.t[:, k * 128:(k + 1) * 128], in_=src.t[:, k * 128:(k + 1) * 128],
            identity=ident.t[:]))
    ctx.mm_group(mms, reads=[src.b, ident.b], writes=[tp.b])
    tpv = tp.t[:, 0:nch * 128].rearrange("p (k t) -> p k t", k=nch)
    if evac_eng is ctx.ACT:
        fn = lambda: nc.scalar.copy(out=dstT.t[:, 0:nch, dst_cols], in_=tpv)
    else:
        fn = lambda: evac_eng.eng.tensor_copy(out=dstT.t[:, 0:nch, dst_cols], in_=tpv)
    ctx.op(evac_eng, fn, reads=[tp.b], writes=[dstT.b], acc=True)


def emit_layernorm(ctx, R, z, gb, out_slot, eps):
    nc = ctx.nc
    st, mv, rs, nm = R["ln_st"], R["ln_mv"], R["ln_rs"], R["ln_nm"]
    ctx.op(ctx.DVE, lambda: nc.vector.bn_stats(out=st.t[:, 0, :], in_=z.t[:, 0:512]),
           reads=[z.b], writes=[st.b])
    ctx.op(ctx.DVE, lambda: nc.vector.bn_stats(out=st.t[:, 1, :], in_=z.t[:, 512:1024]),
           reads=[z.b], writes=[st.b], acc=True)
    ctx.op(ctx.DVE, lambda: nc.vector.bn_aggr(out=mv.t[:], in_=st.t[:].rearrange("p a b -> p (a b)")),
           reads=[st.b], writes=[mv.b])
    ctx.op(ctx.ACT, lambda: nc.scalar.activation(
        out=rs.t[:], in_=mv.t[:, 1:2], func=AF.Sqrt, bias=R[eps].t[:], scale=1.0),
        reads=[mv.b, R[eps].b], writes=[rs.b])
    ctx.op(ctx.DVE, lambda: nc.vector.reciprocal(out=rs.t[:], in_=rs.t[:]),
           reads=[rs.b], writes=[rs.b])
    ctx.op(ctx.DVE, lambda: nc.vector.scalar_tensor_tensor(
        out=nm.t[:], in0=mv.t[:, 0:1], scalar=-1.0, in1=rs.t[:],
        op0=ALU.mult, op1=ALU.mult), reads=[mv.b, rs.b], writes=[nm.b])
    ctx.op(ctx.ACT, lambda: nc.scalar.activation(
        out=z.t[:], in_=z.t[:], func=AF.Identity, bias=nm.t[:], scale=rs.t[:]),
        reads=[z.b, nm.b, rs.b], writes=[z.b])
    ctx.op(ctx.POOL, lambda: nc.gpsimd.tensor_tensor(
        out=z.t[:], in0=z.t[:], in1=gb.t[:, 0, :], op=ALU.mult),
        reads=[z.b, gb.b], writes=[z.b])
    ctx.op(ctx.POOL, lambda: nc.gpsimd.tensor_tensor(
        out=out_slot.t[:], in0=z.t[:], in1=gb.t[:, 1, :], op=ALU.add),
        reads=[z.b, gb.b], writes=[out_slot.b])


def emit_ffn_h(ctx, R, w_in, xT, GT):
    nc = ctx.nc
    wv = w_in.rearrange("(k p) f -> p k f", p=128)
    ws = R["wslots"]

    def load(j):
        s = ws[j % len(ws)]
        ctx.dma(ctx.POOL, s.s, s.t[:, :, 0:128], wv[:, :, j * 128:(j + 1) * 128], writes=[s.b])
        ctx.dma(ctx.POOL, s.s, s.t[:, :, 128:256], wv[:, :, DFF + j * 128:DFF + (j + 1) * 128],
                writes=[s.b], acc=True)

    npre = len(ws) - 1
    for j in range(min(npre, NJ)):
        load(j)
    it = 0
    for j in range(NJ):
        if j + npre < NJ:
            load(j + npre)
        s = ws[j % len(ws)]
        for t in range(NTILE):
            g, u, sg = R["G"][it % 2], R["U"][it % 2], R["sg"][it % 2]
            it += 1
            cols = slice(t * 512, (t + 1) * 512)
            ctx.mm_group([lambda k=k: nc.tensor.matmul(
                g.t[:], lhsT=s.t[:, k, 0:128], rhs=xT.t[:, k, cols],
                start=(k == 0), stop=(k == 7)) for k in range(8)],
                reads=[s.b, xT.b], writes=[g.b])
            ctx.mm_group([lambda k=k: nc.tensor.matmul(
                u.t[:], lhsT=s.t[:, k, 128:256], rhs=xT.t[:, k, cols],
                start=(k == 0), stop=(k == 7)) for k in range(8)],
                reads=[s.b, xT.b], writes=[u.b])
            ctx.op(ctx.ACT, lambda: nc.scalar.activation(out=sg.t[:], in_=g.t[:], func=AF.Silu),
                   reads=[g.b], writes=[sg.b])
            ctx.op(ctx.DVE, lambda: nc.vector.tensor_tensor(
                out=GT.t[:, j, cols], in0=sg.t[:], in1=u.t[:], op=ALU.mult),
                reads=[sg.b, u.b], writes=[GT.b], acc=True)


def emit_load_wout(ctx, R, w_out):
    W = R["Wout"]
    for j in range(NJ):
        ctx.dma(ctx.POOL, W.s, W.t[:, j, :], w_out[j * 128:(j + 1) * 128, :],
                writes=[W.b], acc=(j > 0))


def emit_ffn_y_block(ctx, R, GT, blk, resid, z, c1):
    nc = ctx.nc
    W = R["Wout"]
    ys = R["Y"][blk % 2]
    tok = slice(blk * 128, (blk + 1) * 128)
    for n in range(2):
        y = ys[n]
        ctx.mm_group([lambda j=j: nc.tensor.matmul(
            y.t[:], lhsT=GT.t[:, j, tok], rhs=W.t[:, j, n * 512:(n + 1) * 512],
            start=(j == 0), stop=(j == NJ - 1)) for j in range(NJ)],
            reads=[GT.b, W.b], writes=[y.b])
    for n in range(2):
        y = ys[n]
        cs = slice(n * 512, (n + 1) * 512)
        ctx.op(ctx.DVE, lambda: nc.vector.scalar_tensor_tensor(
            out=z.t[:, cs], in0=y.t[:], scalar=float(c1), in1=resid.t[:, cs],
            op0=ALU.mult, op1=ALU.add), reads=[y.b, resid.b], writes=[z.b], acc=(n > 0))


def load_bcast_rows(ctx, dst, j, dram_row):
    ctx.dma(ctx.SP, dst.s, dst.t[:, j, :], dram_row.partition_broadcast(128),
            writes=[dst.b], acc=True)


def phase1(nc, ctx, io):
    x, w_in, w_out, ln_g, ln_b = io["x"], io["w_in"], io["w_out"], io["ln_g"], io["ln_b"]
    w_mix, conv_wT, ident_d = io["w_mix"], io["conv_wT"], io["ident"]
    o_x1, o_qT, o_kT = io["x1"], io["qT"], io["kT"]
    o_fl, o_cv, o_ul, o_bf = io["fl"], io["convT"], io["ulast"], io["bfirst"]
    OUT = Buf("dram_out")

    with contextlib.ExitStack() as stack:
        A = Res(nc, ctx, stack)
        R = {}
        ident = A.sb("ident", [128, 128], BF16, dma=True)
        R["wslots"] = [A.sb("wsl", [128, 8, 256], BF16, dma=True) for _ in range(3)]
        R["Wout"] = A.sb("Wout", [128, NJ, D], BF16, dma=True)
        GT = A.sb("GT", [128, NJ, HALF], BF16)
        xT = A.sb("xT", [128, 8, HALF], BF16)
        xr = [A.sb("xr", [128, D], F32, dma=True) for _ in range(2)]
        zz = [A.sb("z", [128, D], F32) for _ in range(2)]
        x1s = [A.sb("x1s", [128, D], F32) for _ in range(2)]
        xb = [A.sb("xb", [128, D], BF16) for _ in range(2)]
        gb = A.sb("gb", [128, 2, D], F32, dma=True)
        R["sg"] = [A.sb("sg", [128, 512], F32) for _ in range(2)]
        R["ln_st"] = A.sb("lnst", [128, 2, 6], F32)
        R["ln_mv"] = A.sb("lnmv", [128, 2], F32)
        R["ln_rs"] = A.sb("lnrs", [128, 1], F32)
        R["ln_nm"] = A.sb("lnnm", [128, 1], F32)
        R["eps_ln"] = A.sb("epsln", [128, 1], F32)
        ctx.op(ctx.POOL, lambda: nc.gpsimd.memset(R["eps_ln"].t[:], float(EPS_LN)), writes=[R["eps_ln"].b])
        Wv = A.sb("Wv", [128, 8, 512], BF16, dma=True)
        Wf = A.sb("Wf", [128, 8, 8], BF16, dma=True)
        cw = A.sb("cw", [128, 4, 3], F32, dma=True)
        car = A.sb("car", [128, 4, 2], F32)
        bfs = A.sb("bfs", [128, 4, 2], F32)
        stg = [A.sb("stg", [128, 512], BF16) for _ in range(2)]
        fst = [A.sb("fst", [8, 512], F32) for _ in range(2)]
        cst = [A.sb("cst", [128, 512], F32) for _ in range(2)]
        ut = [A.sb("ut", [128, 514], F32) for _ in range(2)]
        yt = [A.sb("yt", [128, 512], F32) for _ in range(2)]
        banks = [A.ps("bk", [128, 512], F32) for _ in range(6)]
        tps = [A.ps("tp", [128, 1024], BF16) for _ in range(2)]
        R["G"] = [banks[0], banks[1]]
        R["U"] = [banks[2], banks[3]]
        R["Y"] = [(banks[0], banks[2]), (banks[1], banks[3])]
        osem = ctx.dsem("out")
        wvT = w_mix.rearrange("(k p) f -> p k f", p=128)

        ctx.dma(ctx.SP, ident.s, ident.t[:], ident_d, writes=[ident.b])
        load_bcast_rows(ctx, gb, 0, ln_g)
        load_bcast_rows(ctx, gb, 1, ln_b)
        ctx.dma(ctx.SP, cw.s, cw.t[:], conv_wT.rearrange("(c p) j -> p c j", p=128), writes=[cw.b])
        ctx.op(ctx.POOL, lambda: nc.gpsimd.memset(car.t[:], 0.0), writes=[car.b])
        emit_load_wout(ctx, R, w_out)
        ctx.dma(ctx.POOL, Wv.s, Wv.t[:], wvT[:, :, 1024:1536], writes=[Wv.b])
        ctx.dma(ctx.POOL, Wf.s, Wf.t[:], wvT[:, :, 1536:1544], writes=[Wf.b])

        nblk = 0
        ntp = 0
        for hh in range(NHALF):
            t0 = hh * HALF
            for b in range(NBLK):
                r = xr[nblk % 2]
                rows = slice(t0 + b * 128, t0 + (b + 1) * 128)
                ctx.dma(ctx.SP, r.s, r.t[:], x[rows, :], writes=[r.b])
                xbb = xb[nblk % 2]
                ctx.op(ctx.ACT, lambda: nc.scalar.copy(out=xbb.t[:], in_=r.t[:]),
                       reads=[r.b], writes=[xbb.b])
                emit_transposes(ctx, xbb, 8, ident, tps[ntp % 2], xT,
                                slice(b * 128, (b + 1) * 128), ctx.DVE)
                ntp += 1
                nblk += 1
            emit_ffn_h(ctx, R, w_in, xT, GT)
            for b in range(NBLK):
                r = xr[nblk % 2]
                z = zz[nblk % 2]
                x1 = x1s[nblk % 2]
                xbb = xb[nblk % 2]
                rows = slice(t0 + b * 128, t0 + (b + 1) * 128)
                ctx.dma(ctx.SP, r.s, r.t[:], x[rows, :], writes=[r.b])
                emit_ffn_y_block(ctx, R, GT, b, r, z, 0.5 / ALPHA)
                emit_layernorm(ctx, R, z, gb, x1, "eps_ln")
                ctx.dma(ctx.SP, x1.s, o_x1[rows, :], x1.t[:], reads=[x1.b], writes=[OUT], acc=True)
                ctx.op(ctx.ACT, lambda: nc.scalar.copy(out=xbb.t[:], in_=x1.t[:]),
                       reads=[x1.b], writes=[xbb.b])
                emit_transposes(ctx, xbb, 8, ident, tps[ntp % 2], xT,
                                slice(b * 128, (b + 1) * 128), ctx.DVE)
                ntp += 1
                nblk += 1
            ws = R["wslots"]
            wi = 0
            ns = 0
            for name, col0, dst, scale in (("q", 0, o_qT, 0.125), ("k", 512, o_kT, 1.0)):
                for pair in range(2):
                    s = ws[wi % 3]
                    wi += 1
                    c0 = col0 + pair * 256
                    ctx.dma(ctx.POOL, s.s, s.t[:], wvT[:, :, c0:c0 + 256], writes=[s.b])
                    for cc in range(2):
                        ch = pair * 2 + cc
                        for t in range(NTILE):
                            bk = banks[4 + (ns % 2)]
                            sg_ = stg[ns % 2]
                            ns += 1
                            cols = slice(t * 512, (t + 1) * 512)
                            ctx.mm_group([lambda k=k: nc.tensor.matmul(
                                bk.t[:], lhsT=s.t[:, k, cc * 128:(cc + 1) * 128], rhs=xT.t[:, k, cols],
                                start=(k == 0), stop=(k == 7)) for k in range(8)],
                                reads=[s.b, xT.b], writes=[bk.b])
                            ctx.op(ctx.ACT, lambda: nc.scalar.activation(
                                out=sg_.t[:], in_=bk.t[:], func=AF.Copy, scale=float(scale)),
                                reads=[bk.b], writes=[sg_.b])
                            ctx.dma(ctx.SP, sg_.s, dst[ch * 128:(ch + 1) * 128, t0 + t * 512:t0 + (t + 1) * 512],
                                    sg_.t[:], reads=[sg_.b], writes=[OUT], acc=True)
            for b in range(NBLK):
                bk = banks[4 + (ns % 2)]
                sg_ = stg[ns % 2]
                ns += 1
                tok = slice(b * 128, (b + 1) * 128)
                ctx.mm_group([lambda k=k: nc.tensor.matmul(
                    bk.t[:], lhsT=xT.t[:, k, tok], rhs=Wv.t[:, k, :],
                    start=(k == 0), stop=(k == 7)) for k in range(8)],
                    reads=[Wv.b, xT.b], writes=[bk.b])
                ctx.op(ctx.ACT, lambda: nc.scalar.copy(out=sg_.t[:], in_=bk.t[:]),
                       reads=[bk.b], writes=[sg_.b])
                ctx.dma(ctx.SP, sg_.s, io["v_dst"](hh * NBLK + b), sg_.t[:].rearrange("p (h d) -> p h d", d=64),
                        reads=[sg_.b], writes=[OUT], acc=True)
            for t in range(NTILE):
                bk = banks[4 + (ns % 2)]
                fs = fst[ns % 2]
                ns += 1
                cols = slice(t * 512, (t + 1) * 512)
                ctx.mm_group([lambda k=k: nc.tensor.matmul(
                    bk.t[0:8, :], lhsT=Wf.t[:, k, :], rhs=xT.t[:, k, cols],
                    start=(k == 0), stop=(k == 7)) for k in range(8)],
                    reads=[Wf.b, xT.b], writes=[bk.b])
                ctx.op(ctx.ACT, lambda: nc.scalar.copy(out=fs.t[:], in_=bk.t[0:8, :]),
                       reads=[bk.b], writes=[fs.b])
                ctx.dma(ctx.SP, fs.s, o_fl[:, t0 + t * 512:t0 + (t + 1) * 512], fs.t[:],
                        reads=[fs.b], writes=[OUT], acc=True)
            nu = 0
            for c in range(4):
                s1 = ws[wi % 3]
                wi += 1
                s2 = ws[wi % 3]
                wi += 1
                cB, cC, cH = 1544 + c * 128, 2056 + c * 128, 2568 + c * 128
                ctx.dma(ctx.POOL, s1.s, s1.t[:, :, 0:128], wvT[:, :, cB:cB + 128], writes=[s1.b])
                ctx.dma(ctx.POOL, s1.s, s1.t[:, :, 128:256], wvT[:, :, cC:cC + 128], writes=[s1.b], acc=True)
                ctx.dma(ctx.POOL, s2.s, s2.t[:, :, 0:128], wvT[:, :, cH:cH + 128], writes=[s2.b])
                for t in range(NTILE):
                    cols = slice(t * 512, (t + 1) * 512)
                    pB, pC, pH = banks[0 + (nu % 2)], banks[2 + (nu % 2)], banks[4 + (nu % 2)]
                    cs_, u_, y_ = cst[nu % 2], ut[nu % 2], yt[nu % 2]
                    nu += 1
                    for pb, sl, lo in ((pB, s1, 0), (pC, s1, 128), (pH, s2, 0)):
                        ctx.mm_group([lambda k=k, pb=pb, sl=sl, lo=lo: nc.tensor.matmul(
                            pb.t[:], lhsT=sl.t[:, k, lo:lo + 128], rhs=xT.t[:, k, cols],
                            start=(k == 0), stop=(k == 7)) for k in range(8)],
                            reads=[sl.b, xT.b], writes=[pb.b])
                    ctx.op(ctx.ACT, lambda: nc.scalar.copy(out=cs_.t[:], in_=pC.t[:]),
                           reads=[pC.b], writes=[cs_.b])
                    ctx.op(ctx.POOL, lambda: nc.gpsimd.tensor_copy(out=u_.t[:, 0:2], in_=car.t[:, c, :]),
                           reads=[car.b], writes=[u_.b])
                    ctx.op(ctx.DVE, lambda: nc.vector.tensor_tensor(
                        out=u_.t[:, 2:514], in0=cs_.t[:], in1=pH.t[:], op=ALU.mult),
                        reads=[cs_.b, pH.b], writes=[u_.b], acc=True)
                    ctx.op(ctx.POOL, lambda: nc.gpsimd.tensor_copy(out=car.t[:, c, :], in_=u_.t[:, 512:514]),
                           reads=[u_.b], writes=[car.b])
                    if hh == 0 and t == 0:
                        ctx.op(ctx.ACT, lambda: nc.scalar.copy(out=bfs.t[:, c, :], in_=pB.t[:, 0:2]),
                               reads=[pB.b], writes=[bfs.b], acc=True)
                    ctx.op(ctx.DVE, lambda: nc.vector.tensor_scalar(
                        out=y_.t[:], in0=u_.t[:, 2:514], scalar1=cw.t[:, c, 2:3], scalar2=None,
                        op0=ALU.mult), reads=[u_.b, cw.b], writes=[y_.b])
                    ctx.op(ctx.DVE, lambda: nc.vector.scalar_tensor_tensor(
                        out=y_.t[:], in0=u_.t[:, 1:513], scalar=cw.t[:, c, 1:2], in1=y_.t[:],
                        op0=ALU.mult, op1=ALU.add), reads=[u_.b, cw.b, y_.b], writes=[y_.b])
                    ctx.op(ctx.DVE, lambda: nc.vector.scalar_tensor_tensor(
                        out=y_.t[:], in0=u_.t[:, 0:512], scalar=cw.t[:, c, 0:1], in1=y_.t[:],
                        op0=ALU.mult, op1=ALU.add), reads=[u_.b, cw.b, y_.b], writes=[y_.b])
                    ctx.op(ctx.DVE, lambda: nc.vector.tensor_tensor(
                        out=y_.t[:], in0=y_.t[:], in1=pB.t[:], op=ALU.mult),
                        reads=[y_.b, pB.b], writes=[y_.b])
                    ctx.dma(ctx.SP, y_.s, o_cv[c * 128:(c + 1) * 128, t0 + t * 512:t0 + (t + 1) * 512],
                            y_.t[:], reads=[y_.b], writes=[OUT], acc=True)
        ctx.dma(ctx.SP, car.s, o_ul.rearrange("(c p) j -> p c j", p=128), car.t[:],
                reads=[car.b], writes=[OUT], acc=True)
        ctx.dma(ctx.SP, bfs.s, o_bf.rearrange("(c p) j -> p c j", p=128), bfs.t[:],
                reads=[bfs.b], writes=[OUT], acc=True)
        ctx.finish([OUT])


def _ident_bf16():
    return np.eye(128, dtype=np.float32).astype(ml_dtypes.bfloat16)


NQT = SEQ // 512
NKB = SEQ // 128
MASK_NEG = -30000.0
P2_PIPE = True


def phase2(nc, ctx, io):
    fused = io.get("fused", False)
    ident_d, identf_d, triu_d, tris_d = io["ident"], io["identf"], io["triu"], io["tris"]
    ones_d, mask_d, o_oT, cs_d = io["onesf"], io["maskb"], io["oT"], io["cs_scratch"]
    OUT = Buf("dram_out")
    CSD = Buf("cs_dram")

    with contextlib.ExitStack() as stack:
        A = Res(nc, ctx, stack)
        QT = A.sb("QT", [70, SEQ], BF16, dma=True)
        KT = A.sb("KT", [70, SEQ], BF16, dma=True)
        VP = A.sb("VP", [128, NKB, 65], BF16, dma=True)
        ident = A.sb("ident", [128, 128], BF16, dma=True)
        identf = A.sb("identf", [128, 128], F32, dma=True)
        triu = A.sb("triu", [128, 128], F32, dma=True)
        tris = A.sb("tris", [128, 128], F32, dma=True)
        onesf = A.sb("onesf", [128, 128], F32, dma=True)
        maskb = A.sb("maskb", [128, 4, 512], BF16, dma=True)
        fl = A.sb("fl", [128, 128], F32, dma=True)
        bfg = A.sb("bfg", [128, 1], F32, dma=True)
        one1 = A.sb("one1", [128, 1], F32)
        ex = A.sb("ex", [128, 128], F32)
        lf = A.sb("lf", [128, 128], F32)
        xts = A.sb("xts", [128, 128], F32)
        rsb = A.sb("rsb", [128, 128], F32)
        cc = A.sb("cc", [128, 128], F32)
        r1 = A.sb("r1", [128, 128], F32)
        tf = A.sb("tf", [128, 128], F32)
        spl = A.sb("spl", [128, 6, 128], BF16)
        Ps = [A.sb("P", [128, 512], BF16) for _ in range(3)]
        osb = [A.sb("osb", [65, 512], F32) for _ in range(2)]
        Sb = [A.ps("S", [128, 512], F32) for _ in range(4)]
        Ob = [A.ps("O", [128, 512], F32) for _ in range(2)]
        Mb = [A.ps("M", [128, 512], F32) for _ in range(2)]
        osem = ctx.dsem("out")
        csem = ctx.dsem("cs")

        for s_, d_ in ((ident, ident_d), (identf, identf_d), (triu, triu_d), (tris, tris_d),
                       (onesf, ones_d), (maskb, mask_d)):
            ctx.dma(ctx.SP, s_.s, s_.t[:], d_, writes=[s_.b])
        if not fused:
            ctx.dma(ctx.SP, fl.s, fl.t[:], io["fl"], writes=[fl.b])
            ctx.dma(ctx.SP, bfg.s, bfg.t[:], io["bfg"], writes=[bfg.b])
        else:
            pid = nc.sync.partition_id()
            qkv_g, fl_g, fl_loc = io["qkv_g"], io["fl_g"], io["fl_loc"]
            FLL = Buf("fl_loc")
            ctx.dma(ctx.SP, fl.s, fl_loc.rearrange("r (o t) -> r o t", o=1),
                    fl_g.rearrange("(r h) t -> r h t", h=NHEAD)[:, bass.ds(pid, 1), :], writes=[FLL])
            ctx.dma(ctx.SP, fl.s, fl.t[:], fl_loc.rearrange("r (a j) -> (r a) j", j=128),
                    reads=[FLL], writes=[fl.b])
            ctx.dma(ctx.SP, bfg.s, bfg.t[:], io["b_forget"][0:1, bass.ds(pid, 1)].broadcast_to([128, 1]),
                    writes=[bfg.b])
        ctx.op(ctx.POOL, lambda: nc.gpsimd.memset(QT.t[64:70, :], 1.0), writes=[QT.b])
        ctx.op(ctx.POOL, lambda: nc.gpsimd.memset(KT.t[64:70, :], 1.0), writes=[KT.b])
        ctx.op(ctx.POOL, lambda: nc.gpsimd.memset(one1.t[:], 1.0), writes=[one1.b])
        if not fused:
            qT_d, kT_d, v_d = io["qT"], io["kT"], io["v"]
            for q4 in range(4):
                cs = slice(q4 * 4096, (q4 + 1) * 4096)
                ctx.dma(ctx.SP, QT.s, QT.t[0:64, cs], qT_d[:, cs], writes=[QT.b], acc=True)
                ctx.dma(ctx.SP, KT.s, KT.t[0:64, cs], kT_d[:, cs], writes=[KT.b], acc=True)
            ctx.dma(ctx.SP, VP.s, VP.t[:, :, 0:64], v_d, writes=[VP.b])
        else:
            v_g = io["v_g"]
            qv = qkv_g.rearrange("(r a) t -> r a t", r=NCORES)
            ctx.dma(ctx.SP, QT.s, QT.t[0:64, :].rearrange("p (r t) -> p r t", r=NCORES),
                    qv[:, 0:512, :][:, bass.ds(pid * 64, 64), :].rearrange("r p t -> p r t"),
                    writes=[QT.b], acc=True)
            ctx.dma(ctx.SP, KT.s, KT.t[0:64, :].rearrange("p (r t) -> p r t", r=NCORES),
                    qv[:, 512:1024, :][:, bass.ds(pid * 64, 64), :].rearrange("r p t -> p r t"),
                    writes=[KT.b], acc=True)
            vv = v_g.rearrange("(r a) c -> r a c", r=NCORES)
            v_loc = io["v_loc"]
            VLOC = Buf("v_loc")
            ctx.dma(ctx.SP, VP.s, v_loc, vv[:, bass.ds(pid * 128, 128), :], writes=[VLOC])
            for r in range(NCORES):
                ctx.dma(ctx.SP, VP.s, VP.t[:, r * 16:(r + 1) * 16, 0:64],
                        v_loc[r].rearrange("p (kb d) -> p kb d", d=64),
                        reads=[VLOC], writes=[VP.b], acc=(r > 0))
        ctx.op(ctx.POOL, lambda: nc.gpsimd.memset(VP.t[:, :, 64:65], 1.0), writes=[VP.b])

        ctx.op(ctx.DVE, lambda: nc.vector.tensor_scalar(
            out=bfg.t[:], in0=bfg.t[:], scalar1=-1.0, scalar2=None, op0=ALU.mult),
            reads=[bfg.b], writes=[bfg.b])
        ctx.op(ctx.ACT, lambda: nc.scalar.activation(
            out=ex.t[:], in_=fl.t[:], func=AF.Exp, bias=bfg.t[:], scale=-1.0),
            reads=[fl.b, bfg.b], writes=[ex.b])
        ctx.op(ctx.ACT, lambda: nc.scalar.activation(
            out=ex.t[:], in_=ex.t[:], func=AF.Ln, bias=one1.t[:], scale=1.0),
            reads=[ex.b, one1.b], writes=[ex.b])
        ctx.op(ctx.DVE, lambda: nc.vector.tensor_scalar(
            out=lf.t[:], in0=ex.t[:], scalar1=-1.0, scalar2=None, op0=ALU.mult),
            reads=[ex.b], writes=[lf.b])
        m0, m1 = Mb
        ctx.mm_group([lambda: nc.tensor.transpose(out=m0.t[:, 0:128], in_=lf.t[:], identity=identf.t[:])],
                     reads=[lf.b, identf.b], writes=[m0.b])
        ctx.op(ctx.DVE, lambda: nc.vector.tensor_copy(out=xts.t[:], in_=m0.t[:, 0:128]),
               reads=[m0.b], writes=[xts.b])
        ctx.mm_group([lambda: nc.tensor.matmul(m1.t[:, 0:128], lhsT=xts.t[:], rhs=onesf.t[:],
                                               start=True, stop=True)],
                     reads=[xts.b, onesf.b], writes=[m1.b])
        ctx.op(ctx.DVE, lambda: nc.vector.tensor_copy(out=rsb.t[:], in_=m1.t[:, 0:128]),
               reads=[m1.b], writes=[rsb.b])
        ctx.mm_group([
            lambda: nc.tensor.matmul(m0.t[:, 0:128], lhsT=xts.t[:], rhs=triu.t[:], start=True, stop=False),
            lambda: nc.tensor.matmul(m0.t[:, 0:128], lhsT=tris.t[:], rhs=rsb.t[:], start=False, stop=True)],
            reads=[xts.b, triu.b, tris.b, rsb.b], writes=[m0.b])
        ctx.op(ctx.DVE, lambda: nc.vector.tensor_copy(out=cc.t[:], in_=m0.t[:, 0:128]),
               reads=[m0.b], writes=[cc.b])
        src = cc
        for i in range(3):
            ctx.op(ctx.DVE, lambda: nc.vector.tensor_copy(out=spl.t[:, i, :], in_=src.t[:]),
                   reads=[src.b], writes=[spl.b], acc=True)
            ctx.op(ctx.DVE, lambda: nc.vector.tensor_copy(out=tf.t[:], in_=spl.t[:, i, :]),
                   reads=[spl.b], writes=[tf.b])
            ctx.op(ctx.DVE, lambda: nc.vector.tensor_scalar(
                out=spl.t[:, 3 + i, :], in0=tf.t[:], scalar1=-1.0, scalar2=None, op0=ALU.mult),
                reads=[tf.b], writes=[spl.b], acc=True)
            if i < 2:
                ctx.op(ctx.DVE, lambda: nc.vector.tensor_tensor(
                    out=r1.t[:], in0=src.t[:], in1=tf.t[:], op=ALU.subtract),
                    reads=[src.b, tf.b], writes=[r1.b])
                src = r1
        for i in range(6):
            ctx.dma(ctx.SP, spl.s, cs_d[i, :].rearrange("(p j) -> p j", p=128), spl.t[:, i, :],
                    reads=[spl.b], writes=[CSD], acc=True)
        ctx.dma(ctx.SP, QT.s, QT.t[67:70, :], cs_d[0:3, :], reads=[CSD], writes=[QT.b])
        ctx.dma(ctx.SP, KT.s, KT.t[64:67, :], cs_d[3:6, :], reads=[CSD], writes=[KT.b])

        ns = 0
        for qt in range(NQT):
            qc = slice(qt * 512, (qt + 1) * 512)
            O = Ob[qt % 2]
            nkb = 4 * qt + 4

            def s_step(kb):
                S = Sb[(ns_base + kb) % 4]
                kc = slice(kb * 128, (kb + 1) * 128)
                d = kb - 4 * qt
                mms = [lambda: nc.tensor.matmul(S.t[:], lhsT=KT.t[:, kc], rhs=QT.t[:, qc],
                                                start=True, stop=(d < 0))]
                rd = [KT.b, QT.b]
                if d >= 0:
                    mms.append(lambda: nc.tensor.matmul(S.t[:], lhsT=ident.t[:], rhs=maskb.t[:, d, :],
                                                        start=False, stop=True))
                    rd += [ident.b, maskb.b]
                ctx.mm_group(mms, reads=rd, writes=[S.b])

            ns_base = ns
            if P2_PIPE:
                s_step(0)
            for kb in range(nkb):
                if P2_PIPE:
                    if kb + 1 < nkb:
                        s_step(kb + 1)
                else:
                    s_step(kb)
                S = Sb[(ns_base + kb) % 4]
                P = Ps[(ns_base + kb) % 3]
                ctx.op(ctx.ACT, lambda: nc.scalar.activation(out=P.t[:], in_=S.t[:], func=AF.Exp),
                       reads=[S.b], writes=[P.b])
                ctx.mm_group([lambda: nc.tensor.matmul(
                    O.t[0:65, :], lhsT=VP.t[:, kb, :], rhs=P.t[:],
                    start=(kb == 0), stop=(kb == nkb - 1))],
                    reads=[VP.b, P.b], writes=[O.b], acc=(kb > 0))
            ns += nkb
            ob = osb[qt % 2]
            ctx.op(ctx.DVE, lambda: nc.vector.tensor_copy(out=ob.t[:], in_=O.t[0:65, :]),
                   reads=[O.b], writes=[ob.b])
            ctx.dma(ctx.SP, ob.s, o_oT[:, qc], ob.t[:], reads=[ob.b], writes=[OUT], acc=True)
        ctx.finish([OUT])


def _phase2_consts():
    j = np.arange(128)
    triu = (j[:, None] <= j[None, :]).astype(np.float32)
    tris = (j[:, None] < j[None, :]).astype(np.float32)
    k = np.arange(128)[:, None, None]
    d = np.arange(4)[None, :, None]
    q = np.arange(512)[None, None, :]
    mask = np.where(128 * d + k > q, MASK_NEG, 0.0).astype(np.float32).astype(ml_dtypes.bfloat16)
    return dict(ident=_ident_bf16(), identf=np.eye(128, dtype=np.float32), triu=triu, tris=tris,
                onesf=np.ones((128, 128), np.float32), maskb=mask)


def phase3(nc, ctx, io):
    fused = io.get("fused", False)
    x1_d, cv_d, bf_d, cw_d = io["x1"], io["convT"], io["bfirst"], io["conv_wT"]
    ga_d, gc_d, wmo_d = io["g_attn"], io["g_conv"], io["w_mo"]
    ln_d = [(io[f"ln{i}_g"], io[f"ln{i}_b"]) for i in (2, 3, 4)]
    w_in, w_out, p_d, wp_d, wg_d, bg_d = io["w_in"], io["w_out"], io["p"], io["w_ple"], io["w_pg"], io["b_pg"]
    ident_d, o_out, xs2_d, xs3_d = io["ident"], io["out"], io["xs2"], io["xs3"]
    if fused:
        pid = nc.sync.partition_id()
        o_g, o_loc = io["o_g"], io["o_loc"]
        up_d = io["ul_g"][bass.ds(((pid + (NCORES - 1)) % NCORES) * 512, 512), :]
        oT_rows = lambda head, r0, r1, t0: o_loc[head * 65 + r0:head * 65 + r1, t0:t0 + HALF]
    else:
        up_d = io["uprev"]
        oT_d = io["oT"]
        oT_rows = lambda head, r0, r1, t0: oT_d[head, r0:r1, t0:t0 + HALF]
    OUT, XS2, XS3, OLOC = Buf("dram_out"), Buf("xs2"), Buf("xs3"), Buf("o_loc")

    with contextlib.ExitStack() as stack:
        A = Res(nc, ctx, stack)
        R = {}
        ident = A.sb("ident", [128, 128], BF16, dma=True)
        if fused:
            ctx.dma(ctx.SP, ctx.dsem("oloc"), o_loc, o_g[:, bass.ds(pid * T, T)], writes=[OLOC])
        R["wslots"] = [A.sb("wsl", [128, 8, 256], BF16, dma=True) for _ in range(3)]
        Wres = A.sb("Wres", [128, NJ, D], BF16, dma=True)
        R["Wout"] = Wres
        Wg = A.sb("Wg", [128, 8, D], BF16, dma=True)
        Wp = A.sb("Wp", [128, 2, D], BF16, dma=True)
        GT = A.sb("GT", [128, NJ, HALF], BF16)
        xT = A.sb("xT", [128, 8, HALF], BF16)
        pT = A.sb("pT", [128, 2, HALF], BF16)
        Fs = [A.sb("F", [128, D], F32, dma=True) for _ in range(6)]
        gbs = [A.sb("gb", [128, 2, D], F32, dma=True) for _ in range(2)]
        bg = A.sb("bg", [128, 1, D], F32, dma=True)
        xb = [A.sb("xb", [128, D], BF16) for _ in range(2)]
        pf = [A.sb("pf", [128, PLE], F32, dma=True) for _ in range(2)]
        pb = [A.sb("pb", [128, PLE], BF16) for _ in range(2)]
        R["sg"] = [A.sb("sg", [128, 512], F32) for _ in range(2)]
        R["ln_st"] = A.sb("lnst", [128, 2, 6], F32)
        R["ln_mv"] = A.sb("lnmv", [128, 2], F32)
        R["ln_rs"] = A.sb("lnrs", [128, 1], F32)
        R["ln_nm"] = A.sb("lnnm", [128, 1], F32)
        R["eps_ln"] = A.sb("epsln", [128, 1], F32)
        eps_rms = A.sb("epsrms", [128, 1], F32)
        onesb = A.sb("onesb", [128, 1], BF16)
        rr = A.sb("rr", [128, 2], F32)
        ga = A.sb("ga", [128, 4], F32, dma=True)
        gc = A.sb("gc", [128, 4], F32, dma=True)
        up = A.sb("up", [128, 4, 2], F32, dma=True)
        bfr = A.sb("bfr", [128, 4, 2], F32, dma=True)
        cw = A.sb("cw", [128, 4, 3], F32, dma=True)
        fx = A.sb("fx", [128, 4, 2], F32)
        ft = A.sb("ft", [128, 4, 2], F32)
        banks = [A.ps("bk", [128, 512], F32) for _ in range(6)]
        tps = [A.ps("tp", [128, 1024], BF16) for _ in range(2)]
        R["G"] = [banks[0], banks[1]]
        R["U"] = [banks[2], banks[3]]
        R["Y"] = [(banks[0], banks[2]), (banks[1], banks[3])]
        osem = ctx.dsem("out")
        s2sem = ctx.dsem("xs2")
        s3sem = ctx.dsem("xs3")
        wgv = wg_d.rearrange("(k p) f -> p k f", p=128)
        wpv = wp_d.rearrange("(k p) f -> p k f", p=128)
        wmov = wmo_d.rearrange("(k p) f -> p k f", p=128)
        cview = lambda a: a.rearrange("(c p) j -> p c j", p=128)
        mergedT = lambda c, cols: GT.t[:, c, cols]
        sqv = lambda c, cols: GT.t[:, 8 + c, cols]
        state = {"f": 0, "xb": 0, "tp": 0, "gb": 0}

        def nextF():
            s = Fs[state["f"] % len(Fs)]
            state["f"] += 1
            return s

        def load_gb(i):
            s = gbs[state["gb"] % 2]
            state["gb"] += 1
            ctx.dma(ctx.SP, s.s, s.t[:, 0, :], ln_d[i][0].partition_broadcast(128), writes=[s.b])
            ctx.dma(ctx.SP, s.s, s.t[:, 1, :], ln_d[i][1].partition_broadcast(128), writes=[s.b], acc=True)
            return s

        def to_xT(src, nch, dstT, cols, src_is_f32=True):
            xbb = xb[state["xb"] % 2]
            state["xb"] += 1
            ctx.op(ctx.ACT, lambda: nc.scalar.copy(out=xbb.t[:, 0:nch * 128], in_=src.t[:, 0:nch * 128]),
                   reads=[src.b], writes=[xbb.b])
            emit_transposes(ctx, xbb, nch, ident, tps[state["tp"] % 2], dstT, cols, ctx.DVE)
            state["tp"] += 1

        ctx.dma(ctx.SP, ident.s, ident.t[:], ident_d, writes=[ident.b])
        ctx.dma(ctx.SP, bg.s, bg.t[:, 0, :], bg_d.partition_broadcast(128), writes=[bg.b])
        for s_, d_ in ((up, up_d), (bfr, bf_d), (cw, cw_d)):
            ctx.dma(ctx.SP, s_.s, s_.t[:], cview(d_), writes=[s_.b])
        for s_, d_ in ((ga, ga_d), (gc, gc_d)):
            ctx.dma(ctx.SP, s_.s, s_.t[:], d_, writes=[s_.b])
        ctx.op(ctx.POOL, lambda: nc.gpsimd.memset(R["eps_ln"].t[:], float(EPS_LN)), writes=[R["eps_ln"].b])
        ctx.op(ctx.POOL, lambda: nc.gpsimd.memset(eps_rms.t[:], float(RMS_EPS)), writes=[eps_rms.b])
        ctx.op(ctx.POOL, lambda: nc.gpsimd.memset(onesb.t[:], 1.0), writes=[onesb.b])
        ctx.dma(ctx.POOL, Wg.s, Wg.t[:], wgv, writes=[Wg.b])
        ctx.dma(ctx.POOL, Wp.s, Wp.t[:], wpv, writes=[Wp.b])
        V = ctx.DVE
        if fused:
            hm = A.sb("hm", [128, 1], F32)
            ctx.dma(ctx.SP, hm.s, hm.t[:], io["halo_mask"], writes=[hm.b])
            ctx.op(V, lambda: nc.vector.tensor_scalar(
                out=up.t[:], in0=up.t[:], scalar1=hm.t[:, 0:1], scalar2=None, op0=ALU.mult),
                reads=[up.b, hm.b], writes=[up.b])
        ctx.op(V, lambda: nc.vector.tensor_tensor(out=ft.t[:, :, 0:1], in0=up.t[:, :, 0:1], in1=cw.t[:, :, 0:1], op=ALU.mult),
               reads=[up.b, cw.b], writes=[ft.b])
        ctx.op(V, lambda: nc.vector.tensor_tensor(out=ft.t[:, :, 1:2], in0=up.t[:, :, 1:2], in1=cw.t[:, :, 1:2], op=ALU.mult),
               reads=[up.b, cw.b], writes=[ft.b])
        ctx.op(V, lambda: nc.vector.tensor_tensor(out=ft.t[:, :, 0:1], in0=ft.t[:, :, 0:1], in1=ft.t[:, :, 1:2], op=ALU.add),
               reads=[ft.b], writes=[ft.b])
        ctx.op(V, lambda: nc.vector.tensor_tensor(out=fx.t[:, :, 0:1], in0=ft.t[:, :, 0:1], in1=bfr.t[:, :, 0:1], op=ALU.mult),
               reads=[ft.b, bfr.b], writes=[fx.b])
        ctx.op(V, lambda: nc.vector.tensor_tensor(out=ft.t[:, :, 1:2], in0=up.t[:, :, 1:2], in1=cw.t[:, :, 0:1], op=ALU.mult),
               reads=[up.b, cw.b, fx.b], writes=[ft.b])
        ctx.op(V, lambda: nc.vector.tensor_tensor(out=fx.t[:, :, 1:2], in0=ft.t[:, :, 1:2], in1=bfr.t[:, :, 1:2], op=ALU.mult),
               reads=[ft.b, bfr.b], writes=[fx.b])

        for hh in range(NHALF):
            t0 = hh * HALF
            hc = slice(t0, t0 + HALF)
            full = slice(0, HALF)
            ctx.dma(ctx.POOL, Wres.s, Wres.t[:, 0:8, :], wmov, writes=[Wres.b])
            for c in range(4):
                Ab, Lb = nextF(), nextF()
                for hd in range(2):
                    ps_ = slice(hd * 64, (hd + 1) * 64)
                    ctx.dma(ctx.SP, Ab.s, Ab.t[ps_, :], oT_rows(2 * c + hd, 0, 64, t0), reads=[OLOC], writes=[Ab.b], acc=(hd > 0))
                    ctx.dma(ctx.SP, Lb.s, Lb.t[ps_, :], oT_rows(2 * c + hd, 64, 65, t0).broadcast_to([64, HALF]),
                            reads=[OLOC], writes=[Lb.b], acc=(hd > 0))
                ctx.op(ctx.DVE, lambda: nc.vector.reciprocal(out=Lb.t[:], in_=Lb.t[:]), reads=[Lb.b], writes=[Lb.b])
                ctx.op(ctx.DVE, lambda: nc.vector.tensor_tensor(out=Ab.t[:], in0=Ab.t[:], in1=Lb.t[:], op=ALU.mult),
                       reads=[Ab.b, Lb.b], writes=[Ab.b])
                ctx.op(ctx.ACT, lambda: nc.scalar.activation(out=sqv(c, full), in_=Ab.t[:], func=AF.Square),
                       reads=[Ab.b], writes=[GT.b], acc=True)
                ctx.op(ctx.POOL, lambda: nc.gpsimd.tensor_scalar(
                    out=mergedT(c, full), in0=Ab.t[:], scalar1=ga.t[:, c:c + 1], scalar2=None, op0=ALU.mult),
                    reads=[Ab.b, ga.b], writes=[GT.b], acc=True)
            for c in range(4):
                Cb = nextF()
                ctx.dma(ctx.SP, Cb.s, Cb.t[:], cv_d[c * 128:(c + 1) * 128, hc], writes=[Cb.b])
                if hh == 0:
                    ctx.op(ctx.DVE, lambda: nc.vector.tensor_tensor(
                        out=Cb.t[:, 0:2], in0=Cb.t[:, 0:2], in1=fx.t[:, c, :], op=ALU.add),
                        reads=[Cb.b, fx.b], writes=[Cb.b])
                ctx.op(ctx.ACT, lambda: nc.scalar.activation(out=sqv(4 + c, full), in_=Cb.t[:], func=AF.Square),
                       reads=[Cb.b], writes=[GT.b], acc=True)
                ctx.op(ctx.POOL, lambda: nc.gpsimd.tensor_scalar(
                    out=mergedT(4 + c, full), in0=Cb.t[:], scalar1=gc.t[:, c:c + 1], scalar2=None, op0=ALU.mult),
                    reads=[Cb.b, gc.b], writes=[GT.b], acc=True)
            gb = load_gb(0)
            for b in range(NBLK):
                tok = slice(b * 128, (b + 1) * 128)
                rows = slice(t0 + b * 128, t0 + (b + 1) * 128)
                ss = banks[4]
                mms = []
                for grp in range(2):
                    for c in range(4):
                        mms.append(lambda grp=grp, c=c: nc.tensor.matmul(
                            ss.t[:, grp:grp + 1], lhsT=sqv(4 * grp + c, tok), rhs=onesb.t[:],
                            start=(c == 0), stop=(c == 3)))
                ctx.mm_group(mms, reads=[GT.b, onesb.b], writes=[ss.b])
                ctx.op(ctx.ACT, lambda: nc.scalar.activation(
                    out=rr.t[:], in_=ss.t[:, 0:2], func=AF.Sqrt, bias=eps_rms.t[:], scale=1.0 / 512.0),
                    reads=[ss.b, eps_rms.b], writes=[rr.b])
                ctx.op(ctx.DVE, lambda: nc.vector.reciprocal(out=rr.t[:], in_=rr.t[:]), reads=[rr.b], writes=[rr.b])
                for n in range(2):
                    for grp in range(2):
                        pbk = banks[grp * 2 + n]
                        ctx.mm_group([lambda c=c, pbk=pbk: nc.tensor.matmul(
                            pbk.t[:], lhsT=mergedT(4 * grp + c, tok), rhs=Wres.t[:, 4 * grp + c, n * 512:(n + 1) * 512],
                            start=(c == 0), stop=(c == 3)) for c in range(4)],
                            reads=[GT.b, Wres.b], writes=[pbk.b])
                xr, tmp, z, x2 = nextF(), nextF(), nextF(), nextF()
                ctx.dma(ctx.SP, xr.s, xr.t[:], x1_d[rows, :], writes=[xr.b])
                for n in range(2):
                    cs = slice(n * 512, (n + 1) * 512)
                    pa, pc = banks[n], banks[2 + n]
                    ctx.op(ctx.ACT, lambda: nc.scalar.activation(
                        out=tmp.t[:, cs], in_=pa.t[:], func=AF.Copy, scale=rr.t[:, 0:1]),
                        reads=[pa.b, rr.b], writes=[tmp.b], acc=(n > 0))
                    ctx.op(ctx.DVE, lambda: nc.vector.scalar_tensor_tensor(
                        out=tmp.t[:, cs], in0=pc.t[:], scalar=rr.t[:, 1:2], in1=tmp.t[:, cs],
                        op0=ALU.mult, op1=ALU.add), reads=[pc.b, rr.b, tmp.b], writes=[tmp.b], acc=True)
                    ctx.op(ctx.DVE, lambda: nc.vector.scalar_tensor_tensor(
                        out=z.t[:, cs], in0=tmp.t[:, cs], scalar=float(1.0 / ALPHA), in1=xr.t[:, cs],
                        op0=ALU.mult, op1=ALU.add), reads=[tmp.b, xr.b], writes=[z.b], acc=(n > 0))
                emit_layernorm(ctx, R, z, gb, x2, "eps_ln")
                ctx.dma(ctx.SP, x2.s, xs2_d[rows, :], x2.t[:], reads=[x2.b], writes=[XS2], acc=True)
                to_xT(x2, 8, xT, tok)
            emit_load_wout(ctx, R, w_out)
            emit_ffn_h(ctx, R, w_in, xT, GT)
            gb = load_gb(1)
            for b in range(NBLK):
                tok = slice(b * 128, (b + 1) * 128)
                rows = slice(t0 + b * 128, t0 + (b + 1) * 128)
                xr, z, x3 = nextF(), nextF(), nextF()
                ctx.dma(ctx.SP, xr.s, xr.t[:], xs2_d[rows, :], reads=[XS2], writes=[xr.b])
                emit_ffn_y_block(ctx, R, GT, b, xr, z, 0.5 / ALPHA)
                emit_layernorm(ctx, R, z, gb, x3, "eps_ln")
                ctx.dma(ctx.SP, x3.s, xs3_d[rows, :], x3.t[:], reads=[x3.b], writes=[XS3], acc=True)
                to_xT(x3, 8, xT, tok)
            gb = load_gb(2)
            for b in range(NBLK):
                tok = slice(b * 128, (b + 1) * 128)
                rows = slice(t0 + b * 128, t0 + (b + 1) * 128)
                pfs = pf[b % 2]
                ctx.dma(ctx.SP, pfs.s, pfs.t[:], p_d[rows, :], writes=[pfs.b])
                to_xT(pfs, 2, pT, tok)
                for n in range(2):
                    gbk, pbk = banks[n], banks[2 + n]
                    ctx.mm_group([lambda k=k, gbk=gbk: nc.tensor.matmul(
                        gbk.t[:], lhsT=xT.t[:, k, tok], rhs=Wg.t[:, k, n * 512:(n + 1) * 512],
                        start=(k == 0), stop=(k == 7)) for k in range(8)],
                        reads=[xT.b, Wg.b], writes=[gbk.b])
                    ctx.mm_group([lambda k=k, pbk=pbk: nc.tensor.matmul(
                        pbk.t[:], lhsT=pT.t[:, k, tok], rhs=Wp.t[:, k, n * 512:(n + 1) * 512],
                        start=(k == 0), stop=(k == 1)) for k in range(2)],
                        reads=[pT.b, Wp.b], writes=[pbk.b])
                xr, tmp, z, xo = nextF(), nextF(), nextF(), nextF()
                ctx.dma(ctx.SP, xr.s, xr.t[:], xs3_d[rows, :], reads=[XS3], writes=[xr.b])
                for n in range(2):
                    cs = slice(n * 512, (n + 1) * 512)
                    gbk, pbk = banks[n], banks[2 + n]
                    ctx.op(ctx.DVE, lambda: nc.vector.tensor_tensor(
                        out=tmp.t[:, cs], in0=gbk.t[:], in1=bg.t[:, 0, cs], op=ALU.add),
                        reads=[gbk.b, bg.b], writes=[tmp.b], acc=(n > 0))
                    ctx.op(ctx.ACT, lambda: nc.scalar.activation(
                        out=tmp.t[:, cs], in_=tmp.t[:, cs], func=AF.Sigmoid),
                        reads=[tmp.b], writes=[tmp.b], acc=True)
                    ctx.op(ctx.DVE, lambda: nc.vector.tensor_tensor(
                        out=tmp.t[:, cs], in0=tmp.t[:, cs], in1=pbk.t[:], op=ALU.mult),
                        reads=[tmp.b, pbk.b], writes=[tmp.b], acc=True)
                    ctx.op(ctx.DVE, lambda: nc.vector.scalar_tensor_tensor(
                        out=z.t[:, cs], in0=tmp.t[:, cs], scalar=float(1.0 / ALPHA), in1=xr.t[:, cs],
                        op0=ALU.mult, op1=ALU.add), reads=[tmp.b, xr.b], writes=[z.b], acc=(n > 0))
                emit_layernorm(ctx, R, z, gb, xo, "eps_ln")
                ctx.dma(ctx.SP, xo.s, o_out[rows, :], xo.t[:], reads=[xo.b], writes=[OUT], acc=True)
        ctx.finish([OUT])


def _all_gather(nc, sem, count, src, dst):
    nc.gpsimd.collective_compute("AllGather", ALU.bypass, replica_groups=[list(range(NCORES))],
                                 ins=[src.opt()], outs=[dst.opt()]).then_inc(sem, 1)
    nc.gpsimd.wait_ge(sem, count)


def _core_barrier(nc):
    nc.all_engine_barrier()
    nc.all_core_barrier()


def build_fused():
    nc = bass.Bass("TRN2", target_bir_lowering=False, num_devices=NCORES)
    din = lambda n, s, dt=F32: nc.dram_tensor(n, list(s), dt, kind="ExternalInput").ap()
    loc = lambda n, s, dt=F32: nc.dram_tensor(n, list(s), dt).ap()
    shr = lambda n, s, dt=F32: nc.dram_tensor(n, list(s), dt, addr_space="Shared").ap()
    I = dict(
        x=din("x", [T, D]), p=din("p", [T, PLE]),
        w1_in=din("w1_in", [D, 2 * DFF]), w1_out=din("w1_out", [DFF, D]),
        ln1_g=din("ln1_g", [1, D]), ln1_b=din("ln1_b", [1, D]),
        w_mix=din("w_mix", [D, DPROJ]), b_forget=din("b_forget", [1, NHEAD]),
        conv_wT=din("conv_wT", [512, 3]), g_attn=din("g_attn", [128, 4]), g_conv=din("g_conv", [128, 4]),
        w_mo=din("w_mo", [D, D]), ln2_g=din("ln2_g", [1, D]), ln2_b=din("ln2_b", [1, D]),
        w2_in=din("w2_in", [D, 2 * DFF]), w2_out=din("w2_out", [DFF, D]),
        ln3_g=din("ln3_g", [1, D]), ln3_b=din("ln3_b", [1, D]),
        w_ple=din("w_ple", [PLE, D]), w_pg=din("w_pg", [D, D]), b_pg=din("b_pg", [1, D]),
        ln4_g=din("ln4_g", [1, D]), ln4_b=din("ln4_b", [1, D]),
        ident=din("ident", [128, 128], BF16), identf=din("identf", [128, 128]),
        triu=din("triu", [128, 128]), tris=din("tris", [128, 128]), onesf=din("onesf", [128, 128]),
        maskb=din("maskb", [128, 4, 512], BF16), halo_mask=din("halo_mask", [128, 1]),
    )
    out = nc.dram_tensor("out", [T, D], F32, kind="ExternalOutput").ap()
    x1s, cvs, bfs_d = loc("x1_spill", [T, D]), loc("cv_spill", [512, T]), loc("bf_spill", [512, 2])
    xs2, xs3 = loc("xs2", [T, D]), loc("xs3", [T, D])
    qkv_in, fl_in, ul_in = loc("qkv_in", [1024, T], BF16), loc("fl_in", [NHEAD, T]), loc("ul_in", [512, 2])
    v_in = loc("v_in", [NHEAD * 128, 1024], BF16)
    qkv_g = shr("qkv_g", [NCORES * 1024, T], BF16)
    v_g = shr("v_g", [NCORES * NHEAD * 128, 1024], BF16)
    fl_g = shr("fl_g", [NCORES * NHEAD, T])
    ul_g = shr("ul_g", [NCORES * 512, 2])
    o_in = loc("o_in", [65, SEQ])
    o_g = shr("o_g", [NCORES * 65, SEQ])
    cs_scr = loc("cs_scratch", [6, SEQ], BF16)
    fl_loc = loc("fl_loc", [NCORES, T])
    v_loc = loc("v_loc", [NCORES, 128, 1024], BF16)
    o_loc = loc("o_loc", [NCORES * 65, T])
    ccsem = nc.alloc_semaphore("cc_sem")

    ctx = Ctx(nc)
    _core_barrier(nc)
    phase1(nc, ctx, dict(
        x=I["x"], w_in=I["w1_in"], w_out=I["w1_out"], ln_g=I["ln1_g"], ln_b=I["ln1_b"], w_mix=I["w_mix"],
        conv_wT=I["conv_wT"], ident=I["ident"], x1=x1s, qT=qkv_in[0:512, :], kT=qkv_in[512:1024, :],
        v_dst=lambda kb: v_in.rearrange("(h p) (kb d) -> p h kb d", p=128, d=64)[:, :, kb, :], fl=fl_in, convT=cvs,
        ulast=ul_in, bfirst=bfs_d))
    _core_barrier(nc)
    _all_gather(nc, ccsem, 1, qkv_in, qkv_g)
    _all_gather(nc, ccsem, 2, v_in, v_g)
    _all_gather(nc, ccsem, 3, fl_in, fl_g)
    _all_gather(nc, ccsem, 4, ul_in, ul_g)
    _core_barrier(nc)
    phase2(nc, ctx, dict(
        fused=True, ident=I["ident"], identf=I["identf"], triu=I["triu"], tris=I["tris"], onesf=I["onesf"],
        maskb=I["maskb"], oT=o_in, cs_scratch=cs_scr, qkv_g=qkv_g, v_g=v_g, fl_g=fl_g, fl_loc=fl_loc, v_loc=v_loc, b_forget=I["b_forget"]))
    _core_barrier(nc)
    _all_gather(nc, ccsem, 5, o_in, o_g)
    _core_barrier(nc)
    phase3(nc, ctx, dict(
        fused=True, x1=x1s, convT=cvs, bfirst=bfs_d, conv_wT=I["conv_wT"], g_attn=I["g_attn"], g_conv=I["g_conv"],
        w_mo=I["w_mo"], ln2_g=I["ln2_g"], ln2_b=I["ln2_b"], ln3_g=I["ln3_g"], ln3_b=I["ln3_b"],
        ln4_g=I["ln4_g"], ln4_b=I["ln4_b"], w_in=I["w2_in"], w_out=I["w2_out"], p=I["p"], w_ple=I["w_ple"],
        w_pg=I["w_pg"], b_pg=I["b_pg"], ident=I["ident"], out=out, xs2=xs2, xs3=xs3, o_g=o_g, o_loc=o_loc, ul_g=ul_g,
        halo_mask=I["halo_mask"]))
    return nc


def _std(nc):
    din = lambda n, s, dt=F32: nc.dram_tensor(n, list(s), dt, kind="ExternalInput").ap()
    dout = lambda n, s, dt=F32: nc.dram_tensor(n, list(s), dt, kind="ExternalOutput").ap()
    return din, dout


def build_phase1():
    nc = bass.Bass("TRN2", target_bir_lowering=False)
    din, dout = _std(nc)
    v = dout("v", [T, 512], BF16)
    io = dict(x=din("x", [T, D]), w_in=din("w_in", [D, 2 * DFF]), w_out=din("w_out", [DFF, D]),
              ln_g=din("ln_g", [1, D]), ln_b=din("ln_b", [1, D]), w_mix=din("w_mix", [D, DPROJ]),
              conv_wT=din("conv_wT", [512, 3]), ident=din("ident", [128, 128], BF16),
              x1=dout("x1", [T, D]), qT=dout("qT", [512, T], BF16), kT=dout("kT", [512, T], BF16),
              v_dst=lambda kb: v[kb * 128:(kb + 1) * 128, :].rearrange("p (h d) -> p h d", d=64),
              fl=dout("fl", [8, T]), convT=dout("convT", [512, T]), ulast=dout("ulast", [512, 2]),
              bfirst=dout("bfirst", [512, 2]))
    phase1(nc, Ctx(nc), io)
    return nc


def build_phase2():
    nc = bass.Bass("TRN2", target_bir_lowering=False)
    din, dout = _std(nc)
    io = dict(qT=din("qT", [64, SEQ], BF16), kT=din("kT", [64, SEQ], BF16), v=din("v", [128, NKB, 64], BF16),
              fl=din("fl", [128, 128]), bfg=din("bfg", [128, 1]), ident=din("ident", [128, 128], BF16),
              identf=din("identf", [128, 128]), triu=din("triu", [128, 128]), tris=din("tris", [128, 128]),
              onesf=din("onesf", [128, 128]), maskb=din("maskb", [128, 4, 512], BF16),
              oT=dout("oT", [65, SEQ]), cs_scratch=nc.dram_tensor("cs_scratch", [6, SEQ], BF16).ap())
    phase2(nc, Ctx(nc), io)
    return nc


def build_phase3():
    nc = bass.Bass("TRN2", target_bir_lowering=False)
    din, dout = _std(nc)
    io = dict(x1=din("x1", [T, D]), oT=din("oT", [NHEAD, 65, T]), convT=din("convT", [512, T]),
              uprev=din("uprev", [512, 2]), bfirst=din("bfirst", [512, 2]), conv_wT=din("conv_wT", [512, 3]),
              g_attn=din("g_attn", [128, 4]), g_conv=din("g_conv", [128, 4]), w_mo=din("w_mo", [D, D]),
              w_in=din("w_in", [D, 2 * DFF]), w_out=din("w_out", [DFF, D]), p=din("p", [T, PLE]),
              w_ple=din("w_ple", [PLE, D]), w_pg=din("w_pg", [D, D]), b_pg=din("b_pg", [1, D]),
              ident=din("ident", [128, 128], BF16), out=dout("out", [T, D]),
              xs2=nc.dram_tensor("xs2", [T, D], F32).ap(), xs3=nc.dram_tensor("xs3", [T, D], F32).ap())
    for i in (2, 3, 4):
        io[f"ln{i}_g"] = din(f"ln{i}_g", [1, D])
        io[f"ln{i}_b"] = din(f"ln{i}_b", [1, D])
    phase3(nc, Ctx(nc), io)
    return nc


def _kernel_unfused(I, x2d, p2d):
    cores = list(range(NCORES))
    ident = I["ident"]
    if "p1" not in _CACHE:
        _CACHE["p1"], _CACHE["p2"], _CACHE["p3"] = build_phase1(), build_phase2(), build_phase3()
    maps = [dict(x=np.ascontiguousarray(x2d[c * T:(c + 1) * T]), w_in=I["w1_in"], w_out=I["w1_out"],
                 ln_g=I["ln1_g"], ln_b=I["ln1_b"], w_mix=I["w_mix"], conv_wT=I["conv_wT"], ident=ident)
            for c in cores]
    r1 = run_bass_kernel_spmd(_CACHE["p1"], maps, core_ids=cores).results
    qT = np.concatenate([r1[c]["qT"] for c in cores], axis=1)
    kT = np.concatenate([r1[c]["kT"] for c in cores], axis=1)
    v = np.concatenate([r1[c]["v"] for c in cores], axis=0)
    fl = np.concatenate([r1[c]["fl"] for c in cores], axis=1)
    consts = {k: I[k] for k in ("ident", "identf", "triu", "tris", "onesf", "maskb")}
    maps = []
    for h in cores:
        vh = v[:, h * 64:(h + 1) * 64].reshape(NKB, 128, 64).transpose(1, 0, 2)
        maps.append(dict(qT=np.ascontiguousarray(qT[h * 64:(h + 1) * 64]), kT=np.ascontiguousarray(kT[h * 64:(h + 1) * 64]),
                         v=np.ascontiguousarray(vh), fl=np.ascontiguousarray(fl[h].reshape(128, 128)),
                         bfg=np.full((128, 1), I["b_forget"][0, h], np.float32), **consts))
    r2 = run_bass_kernel_spmd(_CACHE["p2"], maps, core_ids=cores).results
    oT = np.stack([r2[h]["oT"] for h in cores], axis=0)
    maps = []
    for c in cores:
        uprev = r1[c - 1]["ulast"] if c > 0 else np.zeros((512, 2), np.float32)
        m = dict(x1=r1[c]["x1"], oT=np.ascontiguousarray(oT[:, :, c * T:(c + 1) * T]), convT=r1[c]["convT"],
                 uprev=np.ascontiguousarray(uprev), bfirst=r1[c]["bfirst"], conv_wT=I["conv_wT"],
                 g_attn=I["g_attn"], g_conv=I["g_conv"], w_mo=I["w_mo"], w_in=I["w2_in"], w_out=I["w2_out"],
                 p=np.ascontiguousarray(p2d[c * T:(c + 1) * T]), w_ple=I["w_ple"], w_pg=I["w_pg"], b_pg=I["b_pg"],
                 ident=ident)
        for i in (2, 3, 4):
            m[f"ln{i}_g"], m[f"ln{i}_b"] = I[f"ln{i}_g"], I[f"ln{i}_b"]
        maps.append(m)
    r3 = run_bass_kernel_spmd(_CACHE["p3"], maps, core_ids=cores).results
    if _DBG is not None:
        _DBG.update(r1=r1, r2=r2, r3=r3)
    return np.concatenate([r3[c]["out"] for c in cores], axis=0)


_CACHE = {}
_DBG = None
FUSED = False


def kernel(x, p, ffn1_w_in, ffn1_w_out, ln1_g, ln1_b, w_mix_in, b_forget, conv_w,
           g_attn, g_conv, w_mix_out, ln2_g, ln2_b, ffn2_w_in, ffn2_w_out, ln3_g, ln3_b,
           w_ple, w_ple_gate, b_ple_gate, ln4_g, ln4_b):
    f32 = lambda a: np.ascontiguousarray(np.asarray(a, dtype=np.float32))
    x2d = f32(x)[0]
    p2d = f32(p)[0, 0]
    cores = list(range(NCORES))
    consts = _phase2_consts()
    shared = dict(
        w1_in=f32(ffn1_w_in)[0], w1_out=f32(ffn1_w_out)[0], ln1_g=f32(ln1_g), ln1_b=f32(ln1_b),
        w_mix=f32(w_mix_in)[0], b_forget=f32(b_forget).reshape(1, NHEAD),
        conv_wT=np.ascontiguousarray(f32(conv_w)[0].T),
        g_attn=np.ascontiguousarray(f32(g_attn).reshape(4, 128).T),
        g_conv=np.ascontiguousarray(f32(g_conv).reshape(4, 128).T),
        w_mo=f32(w_mix_out)[0], ln2_g=f32(ln2_g), ln2_b=f32(ln2_b),
        w2_in=f32(ffn2_w_in)[0], w2_out=f32(ffn2_w_out)[0], ln3_g=f32(ln3_g), ln3_b=f32(ln3_b),
        w_ple=f32(w_ple)[0], w_pg=f32(w_ple_gate)[0], b_pg=f32(b_ple_gate),
        ln4_g=f32(ln4_g), ln4_b=f32(ln4_b), **consts)
    if not FUSED:
        out = _kernel_unfused(shared, x2d, p2d)
        return out.reshape(1, SEQ, D).astype(np.float32)
    maps = []
    for c in cores:
        m = dict(shared)
        m["x"] = np.ascontiguousarray(x2d[c * T:(c + 1) * T])
        m["p"] = np.ascontiguousarray(p2d[c * T:(c + 1) * T])
        m["halo_mask"] = np.full((128, 1), 0.0 if c == 0 else 1.0, np.float32)
        maps.append(m)
    if "fused" not in _CACHE:
        _CACHE["fused"] = build_fused()
    res = run_bass_kernel_spmd(_CACHE["fused"], maps, core_ids=cores).results
    out = np.concatenate([res[c]["out"] for c in cores], axis=0)
    return out.reshape(1, SEQ, D).astype(np.float32)
```

```python
import contextlib
import numpy as np
import ml_dtypes
import concourse.bass as bass
import concourse.mybir as mybir
from concourse.bass_utils import run_bass_kernel_spmd

F32 = mybir.dt.float32
BF16 = mybir.dt.bfloat16
ALU = mybir.AluOpType
AF = mybir.ActivationFunctionType

NCORES = 8
SEQ = 16384
D = 1024
DFF = 2816
NJ = DFF // 128
T = SEQ // NCORES
HALF = 1024
NHALF = T // HALF
NBLK = HALF // 128
NTILE = HALF // 512
DPROJ = 3080
NHEAD = 8
PLE = 256
ALPHA = 2.0 ** 0.25
LN_EPS = 1e-5
RMS_EPS = 1e-6
EPS_LN = LN_EPS / (ALPHA * ALPHA)

SAME_ENGINE_SYNC = True


class Buf:
    def __init__(self, name):
        self.name = name
        self.w = {}
        self.r = {}

    @staticmethod
    def _add(d, tok):
        sem, val = tok
        k = id(sem)
        if k not in d or d[k][1] < val:
            d[k] = (sem, val)


class Eng:
    def __init__(self, ctx, eng, name):
        self.eng = eng
        self.name = name
        self.sem = ctx.nc.alloc_semaphore(name + "_prog")
        self.count = 0
        self.waited = {}

    def wait(self, toks):
        for sem, val in toks:
            k = id(sem)
            if sem is self.sem and not SAME_ENGINE_SYNC:
                continue
            if self.waited.get(k, 0) >= val:
                continue
            self.eng.wait_ge(sem, val)
            self.waited[k] = val


class DmaSem:
    def __init__(self, ctx, name):
        self.sem = ctx.nc.alloc_semaphore(name)
        self.count = 0


class Ctx:
    def __init__(self, nc):
        self.nc = nc
        self.PE = Eng(self, nc.tensor, "pe")
        self.ACT = Eng(self, nc.scalar, "act")
        self.DVE = Eng(self, nc.vector, "dve")
        self.POOL = Eng(self, nc.gpsimd, "pool")
        self.SP = Eng(self, nc.sync, "sp")
        self._nsem = 0

    def dsem(self, name):
        self._nsem += 1
        return DmaSem(self, f"{name}_{self._nsem}")

    def _deps(self, reads, writes, acc):
        deps = []
        for b in reads:
            deps += list(b.w.values())
        for b in writes:
            deps += list(b.r.values())
            if not acc:
                deps += list(b.w.values())
        return deps

    def _commit(self, tok, reads, writes, acc):
        for b in reads:
            Buf._add(b.r, tok)
        for b in writes:
            if acc:
                Buf._add(b.w, tok)
            else:
                b.w = {}
                Buf._add(b.w, tok)
                b.r = {}

    def op(self, E, fn, reads=(), writes=(), acc=False):
        E.wait(self._deps(reads, writes, acc))
        ins = fn()
        E.count += 1
        ins.then_inc(E.sem, 1)
        tok = (E.sem, E.count)
        self._commit(tok, reads, writes, acc)
        return tok

    def mm_group(self, mms, reads=(), writes=(), acc=False):
        E = self.PE
        E.wait(self._deps(reads, writes, acc))
        ins = None
        for fn in mms:
            ins = fn()
        E.count += 1
        ins.then_inc(E.sem, 1)
        tok = (E.sem, E.count)
        self._commit(tok, reads, writes, acc)
        return tok

    def dma(self, Q, dsem, out, in_, reads=(), writes=(), acc=False):
        Q.wait(self._deps(reads, writes, acc))
        ins = Q.eng.dma_start(out=out, in_=in_)
        dsem.count += 16
        ins.then_inc(dsem.sem, 16)
        tok = (dsem.sem, dsem.count)
        self._commit(tok, reads, writes, acc)
        return tok

    def finish(self, bufs):
        toks = []
        for b in bufs:
            toks += list(b.w.values())
        self.SP.wait(toks)


class Slot:
    def __init__(self, ctx, t, name, dma=False):
        self.t = t
        self.b = Buf(name)
        self._ctx = ctx
        self._s = None

    @property
    def s(self):
        if self._s is None:
            self._s = self._ctx.dsem(self.b.name)
        return self._s


class Res:
    _uid = [0]

    def __init__(self, nc, ctx, stack):
        self.nc, self.ctx, self.stack = nc, ctx, stack

    def _name(self, name):
        Res._uid[0] += 1
        return f"{name}_{Res._uid[0]}"

    def sb(self, name, shape, dt, dma=False):
        t = self.stack.enter_context(self.nc.sbuf_tensor(self._name(name), list(shape), dt))
        return Slot(self.ctx, t, name, dma)

    def ps(self, name, shape, dt):
        t = self.stack.enter_context(self.nc.psum_tensor(self._name(name), list(shape), dt))
        return Slot(self.ctx, t, name)


def emit_transposes(ctx, src, nch, ident, tp, dstT, dst_cols, evac_eng):
    nc = ctx.nc
    mms = []
    for k in range(nch):
        mms.append(lambda k=k: nc.tensor.transpose(
            out=tp.t[:, k * 128:(k + 1) * 128], in_=src.t[:, k * 128:(k + 1) * 128],
            identity=ident.t[:]))
    ctx.mm_group(mms, reads=[src.b, ident.b], writes=[tp.b])
    tpv = tp.t[:, 0:nch * 128].rearrange("p (k t) -> p k t", k=nch)
    if evac_eng is ctx.ACT:
        fn = lambda: nc.scalar.copy(out=dstT.t[:, 0:nch, dst_cols], in_=tpv)
    else:
        fn = lambda: evac_eng.eng.tensor_copy(out=dstT.t[:, 0:nch, dst_cols], in_=tpv)
    ctx.op(evac_eng, fn, reads=[tp.b], writes=[dstT.b], acc=True)


def emit_layernorm(ctx, R, z, gb, out_slot, eps):
    nc = ctx.nc
    st, mv, rs, nm = R["ln_st"], R["ln_mv"], R["ln_rs"], R["ln_nm"]
    ctx.op(ctx.DVE, lambda: nc.vector.bn_stats(out=st.t[:, 0, :], in_=z.t[:, 0:512]),
           reads=[z.b], writes=[st.b])
    ctx.op(ctx.DVE, lambda: nc.vector.bn_stats(out=st.t[:, 1, :], in_=z.t[:, 512:1024]),
           reads=[z.b], writes=[st.b], acc=True)
    ctx.op(ctx.DVE, lambda: nc.vector.bn_aggr(out=mv.t[:], in_=st.t[:].rearrange("p a b -> p (a b)")),
           reads=[st.b], writes=[mv.b])
    ctx.op(ctx.ACT, lambda: nc.scalar.activation(
        out=rs.t[:], in_=mv.t[:, 1:2], func=AF.Sqrt, bias=R[eps].t[:], scale=1.0),
        reads=[mv.b, R[eps].b], writes=[rs.b])
    ctx.op(ctx.DVE, lambda: nc.vector.reciprocal(out=rs.t[:], in_=rs.t[:]),
           reads=[rs.b], writes=[rs.b])
    ctx.op(ctx.DVE, lambda: nc.vector.scalar_tensor_tensor(
        out=nm.t[:], in0=mv.t[:, 0:1], scalar=-1.0, in1=rs.t[:],
        op0=ALU.mult, op1=ALU.mult), reads=[mv.b, rs.b], writes=[nm.b])
    ctx.op(ctx.ACT, lambda: nc.scalar.activation(
        out=z.t[:], in_=z.t[:], func=AF.Identity, bias=nm.t[:], scale=rs.t[:]),
        reads=[z.b, nm.b, rs.b], writes=[z.b])
    ctx.op(ctx.DVE, lambda: nc.vector.tensor_tensor(
        out=z.t[:], in0=z.t[:], in1=gb.t[:, 0, :], op=ALU.mult),
        reads=[z.b, gb.b], writes=[z.b])
    ctx.op(ctx.POOL, lambda: nc.gpsimd.tensor_tensor(
        out=out_slot.t[:], in0=z.t[:], in1=gb.t[:, 1, :], op=ALU.add),
        reads=[z.b, gb.b], writes=[out_slot.b])


def emit_ffn_h(ctx, R, w_in, xT, GT):
    nc = ctx.nc
    wv = w_in.rearrange("(k p) f -> p k f", p=128)
    ws = R["wslots"]

    def load(j):
        s = ws[j % len(ws)]
        ctx.dma(ctx.POOL, s.s, s.t[:, :, 0:128], wv[:, :, j * 128:(j + 1) * 128], writes=[s.b])
        ctx.dma(ctx.POOL, s.s, s.t[:, :, 128:256], wv[:, :, DFF + j * 128:DFF + (j + 1) * 128],
                writes=[s.b], acc=True)

    npre = len(ws) - 1
    for j in range(min(npre, NJ)):
        load(j)
    it = 0
    for j in range(NJ):
        if j + npre < NJ:
            load(j + npre)
        s = ws[j % len(ws)]
        for t in range(NTILE):
            g, u, sg = R["G"][it % 2], R["U"][it % 2], R["sg"][it % 2]
            it += 1
            cols = slice(t * 512, (t + 1) * 512)
            ctx.mm_group([lambda k=k: nc.tensor.matmul(
                g.t[:], lhsT=s.t[:, k, 0:128], rhs=xT.t[:, k, cols],
                start=(k == 0), stop=(k == 7)) for k in range(8)],
                reads=[s.b, xT.b], writes=[g.b])
            ctx.mm_group([lambda k=k: nc.tensor.matmul(
                u.t[:], lhsT=s.t[:, k, 128:256], rhs=xT.t[:, k, cols],
                start=(k == 0), stop=(k == 7)) for k in range(8)],
                reads=[s.b, xT.b], writes=[u.b])
            ctx.op(ctx.ACT, lambda: nc.scalar.activation(out=sg.t[:], in_=g.t[:], func=AF.Silu),
                   reads=[g.b], writes=[sg.b])
            ctx.op(ctx.DVE, lambda: nc.vector.tensor_tensor(
                out=GT.t[:, j, cols], in0=sg.t[:], in1=u.t[:], op=ALU.mult),
                reads=[sg.b, u.b], writes=[GT.b], acc=True)


def emit_load_wout(ctx, R, w_out):
    W = R["Wout"]
    for j in range(NJ):
        ctx.dma(ctx.POOL, W.s, W.t[:, j, :], w_out[j * 128:(j + 1) * 128, :],
                writes=[W.b], acc=(j > 0))


def emit_ffn_y_mm(ctx, R, GT, blk):
    nc = ctx.nc
    W = R["Wout"]
    ys = R["Y"][blk % 2]
    tok = slice(blk * 128, (blk + 1) * 128)
    for n in range(2):
        y = ys[n]
        ctx.mm_group([lambda j=j: nc.tensor.matmul(
            y.t[:], lhsT=GT.t[:, j, tok], rhs=W.t[:, j, n * 512:(n + 1) * 512],
            start=(j == 0), stop=(j == NJ - 1)) for j in range(NJ)],
            reads=[GT.b, W.b], writes=[y.b])


def emit_ffn_y_combine(ctx, R, blk, resid, z, c1):
    nc = ctx.nc
    ys = R["Y"][blk % 2]
    for n in range(2):
        y = ys[n]
        cs = slice(n * 512, (n + 1) * 512)
        ctx.op(ctx.DVE, lambda: nc.vector.scalar_tensor_tensor(
            out=z.t[:, cs], in0=y.t[:], scalar=float(c1), in1=resid.t[:, cs],
            op0=ALU.mult, op1=ALU.add), reads=[y.b, resid.b], writes=[z.b], acc=(n > 0))


def load_bcast_rows(ctx, dst, j, dram_row):
    ctx.dma(ctx.SP, dst.s, dst.t[:, j, :], dram_row.partition_broadcast(128),
            writes=[dst.b], acc=True)


def phase1(nc, ctx, io):
    x, w_in, w_out, ln_g, ln_b = io["x"], io["w_in"], io["w_out"], io["ln_g"], io["ln_b"]
    w_mix, conv_wT, ident_d = io["w_mix"], io["conv_wT"], io["ident"]
    o_x1, o_qT, o_kT = io["x1"], io["qT"], io["kT"]
    o_fl, o_cv, o_ul, o_bf = io["fl"], io["convT"], io["ulast"], io["bfirst"]
    OUT = Buf("dram_out")

    with contextlib.ExitStack() as stack:
        A = Res(nc, ctx, stack)
        R = {}
        ident = A.sb("ident", [128, 128], BF16, dma=True)
        R["wslots"] = [A.sb("wsl", [128, 8, 256], BF16, dma=True) for _ in range(3)]
        R["Wout"] = A.sb("Wout", [128, NJ, D], BF16, dma=True)
        GT = A.sb("GT", [128, NJ, HALF], BF16)
        xT = A.sb("xT", [128, 8, HALF], BF16)
        xr = [A.sb("xr", [128, D], F32, dma=True) for _ in range(2)]
        zz = [A.sb("z", [128, D], F32) for _ in range(2)]
        x1s = [A.sb("x1s", [128, D], F32) for _ in range(2)]
        xb = [A.sb("xb", [128, D], BF16) for _ in range(2)]
        gb = A.sb("gb", [128, 2, D], F32, dma=True)
        R["sg"] = [A.sb("sg", [128, 512], F32) for _ in range(2)]
        R["ln_st"] = A.sb("lnst", [128, 2, 6], F32)
        R["ln_mv"] = A.sb("lnmv", [128, 2], F32)
        R["ln_rs"] = A.sb("lnrs", [128, 1], F32)
        R["ln_nm"] = A.sb("lnnm", [128, 1], F32)
        R["eps_ln"] = A.sb("epsln", [128, 1], F32)
        ctx.op(ctx.POOL, lambda: nc.gpsimd.memset(R["eps_ln"].t[:], float(EPS_LN)), writes=[R["eps_ln"].b])
        Wv = A.sb("Wv", [128, 8, 512], BF16, dma=True)
        Wf = A.sb("Wf", [128, 8, 8], BF16, dma=True)
        cw = A.sb("cw", [128, 4, 3], F32, dma=True)
        car = A.sb("car", [128, 4, 2], F32)
        bfs = A.sb("bfs", [128, 4, 2], F32)
        stg = [A.sb("stg", [128, 512], BF16) for _ in range(2)]
        fst = [A.sb("fst", [8, 512], F32) for _ in range(2)]
        cst = [A.sb("cst", [128, 512], F32) for _ in range(2)]
        ut = [A.sb("ut", [128, 514], F32) for _ in range(2)]
        yt = [A.sb("yt", [128, 512], F32) for _ in range(2)]
        banks = [A.ps("bk", [128, 512], F32) for _ in range(6)]
        tps = [A.ps("tp", [128, 1024], BF16) for _ in range(2)]
        R["G"] = [banks[0], banks[1]]
        R["U"] = [banks[2], banks[3]]
        R["Y"] = [(banks[0], banks[2]), (banks[1], banks[3])]
        osem = ctx.dsem("out")
        wvT = w_mix.rearrange("(k p) f -> p k f", p=128)

        ctx.dma(ctx.SP, ident.s, ident.t[:], ident_d, writes=[ident.b])
        load_bcast_rows(ctx, gb, 0, ln_g)
        load_bcast_rows(ctx, gb, 1, ln_b)
        ctx.dma(ctx.SP, cw.s, cw.t[:], conv_wT.rearrange("(c p) j -> p c j", p=128), writes=[cw.b])
        ctx.op(ctx.POOL, lambda: nc.gpsimd.memset(car.t[:], 0.0), writes=[car.b])
        emit_load_wout(ctx, R, w_out)
        ctx.dma(ctx.POOL, Wv.s, Wv.t[:], wvT[:, :, 1024:1536], writes=[Wv.b])
        ctx.dma(ctx.POOL, Wf.s, Wf.t[:], wvT[:, :, 1536:1544], writes=[Wf.b])

        nblk = 0
        ntp = 0
        for hh in range(NHALF):
            t0 = hh * HALF
            for b in range(NBLK):
                r = xr[nblk % 2]
                rows = slice(t0 + b * 128, t0 + (b + 1) * 128)
                ctx.dma(ctx.SP, r.s, r.t[:], x[rows, :], writes=[r.b])
                xbb = xb[nblk % 2]
                ctx.op(ctx.ACT, lambda: nc.scalar.copy(out=xbb.t[:], in_=r.t[:]),
                       reads=[r.b], writes=[xbb.b])
                emit_transposes(ctx, xbb, 8, ident, tps[ntp % 2], xT,
                                slice(b * 128, (b + 1) * 128), ctx.DVE)
                ntp += 1
                nblk += 1
            emit_ffn_h(ctx, R, w_in, xT, GT)
            emit_ffn_y_mm(ctx, R, GT, 0)
            for b in range(NBLK):
                r = xr[nblk % 2]
                z = zz[nblk % 2]
                x1 = x1s[nblk % 2]
                xbb = xb[nblk % 2]
                rows = slice(t0 + b * 128, t0 + (b + 1) * 128)
                ctx.dma(ctx.SP, r.s, r.t[:], x[rows, :], writes=[r.b])
                if b + 1 < NBLK:
                    emit_ffn_y_mm(ctx, R, GT, b + 1)
                emit_ffn_y_combine(ctx, R, b, r, z, 0.5 / ALPHA)
                emit_layernorm(ctx, R, z, gb, x1, "eps_ln")
                ctx.dma(ctx.SP, x1.s, o_x1[rows, :], x1.t[:], reads=[x1.b], writes=[OUT], acc=True)
                ctx.op(ctx.ACT, lambda: nc.scalar.copy(out=xbb.t[:], in_=x1.t[:]),
                       reads=[x1.b], writes=[xbb.b])
                emit_transposes(ctx, xbb, 8, ident, tps[ntp % 2], xT,
                                slice(b * 128, (b + 1) * 128), ctx.DVE)
                ntp += 1
                nblk += 1
            ws = R["wslots"]
            wi = 0
            ns = 0
            for name, col0, dst, scale in (("q", 0, o_qT, 0.125), ("k", 512, o_kT, 1.0)):
                for pair in range(2):
                    s = ws[wi % 3]
                    wi += 1
                    c0 = col0 + pair * 256
                    ctx.dma(ctx.POOL, s.s, s.t[:], wvT[:, :, c0:c0 + 256], writes=[s.b])
                    for cc in range(2):
                        ch = pair * 2 + cc
                        for t in range(NTILE):
                            bk = banks[4 + (ns % 2)]
                            sg_ = stg[ns % 2]
                            ns += 1
                            cols = slice(t * 512, (t + 1) * 512)
                            ctx.mm_group([lambda k=k: nc.tensor.matmul(
                                bk.t[:], lhsT=s.t[:, k, cc * 128:(cc + 1) * 128], rhs=xT.t[:, k, cols],
                                start=(k == 0), stop=(k == 7)) for k in range(8)],
                                reads=[s.b, xT.b], writes=[bk.b])
                            ctx.op(ctx.ACT, lambda: nc.scalar.activation(
                                out=sg_.t[:], in_=bk.t[:], func=AF.Copy, scale=float(scale)),
                                reads=[bk.b], writes=[sg_.b])
                            ctx.dma(ctx.SP, sg_.s, dst[ch * 128:(ch + 1) * 128, t0 + t * 512:t0 + (t + 1) * 512],
                                    sg_.t[:], reads=[sg_.b], writes=[OUT], acc=True)
            for b in range(NBLK):
                bk = banks[4 + (ns % 2)]
                sg_ = stg[ns % 2]
                ns += 1
                tok = slice(b * 128, (b + 1) * 128)
                ctx.mm_group([lambda k=k: nc.tensor.matmul(
                    bk.t[:], lhsT=xT.t[:, k, tok], rhs=Wv.t[:, k, :],
                    start=(k == 0), stop=(k == 7)) for k in range(8)],
                    reads=[Wv.b, xT.b], writes=[bk.b])
                ctx.op(ctx.ACT, lambda: nc.scalar.copy(out=sg_.t[:], in_=bk.t[:]),
                       reads=[bk.b], writes=[sg_.b])
                ctx.dma(ctx.SP, sg_.s, io["v_dst"](hh * NBLK + b), sg_.t[:].rearrange("p (h d) -> p h d", d=64),
                        reads=[sg_.b], writes=[OUT], acc=True)
            for t in range(NTILE):
                bk = banks[4 + (ns % 2)]
                fs = fst[ns % 2]
                ns += 1
                cols = slice(t * 512, (t + 1) * 512)
                ctx.mm_group([lambda k=k: nc.tensor.matmul(
                    bk.t[0:8, :], lhsT=Wf.t[:, k, :], rhs=xT.t[:, k, cols],
                    start=(k == 0), stop=(k == 7)) for k in range(8)],
                    reads=[Wf.b, xT.b], writes=[bk.b])
                ctx.op(ctx.ACT, lambda: nc.scalar.copy(out=fs.t[:], in_=bk.t[0:8, :]),
                       reads=[bk.b], writes=[fs.b])
                ctx.dma(ctx.SP, fs.s, o_fl[:, t0 + t * 512:t0 + (t + 1) * 512], fs.t[:],
                        reads=[fs.b], writes=[OUT], acc=True)
            nu = 0
            for c in range(4):
                s1 = ws[wi % 3]
                wi += 1
                s2 = ws[wi % 3]
                wi += 1
                cB, cC, cH = 1544 + c * 128, 2056 + c * 128, 2568 + c * 128
                ctx.dma(ctx.POOL, s1.s, s1.t[:, :, 0:128], wvT[:, :, cB:cB + 128], writes=[s1.b])
                ctx.dma(ctx.POOL, s1.s, s1.t[:, :, 128:256], wvT[:, :, cC:cC + 128], writes=[s1.b], acc=True)
                ctx.dma(ctx.POOL, s2.s, s2.t[:, :, 0:128], wvT[:, :, cH:cH + 128], writes=[s2.b])
                for t in range(NTILE):
                    cols = slice(t * 512, (t + 1) * 512)
                    pB, pC, pH = banks[0 + (nu % 2)], banks[2 + (nu % 2)], banks[4 + (nu % 2)]
                    cs_, u_, y_ = cst[nu % 2], ut[nu % 2], yt[nu % 2]
                    nu += 1
                    for pb, sl, lo in ((pB, s1, 0), (pC, s1, 128), (pH, s2, 0)):
                        ctx.mm_group([lambda k=k, pb=pb, sl=sl, lo=lo: nc.tensor.matmul(
                            pb.t[:], lhsT=sl.t[:, k, lo:lo + 128], rhs=xT.t[:, k, cols],
                            start=(k == 0), stop=(k == 7)) for k in range(8)],
                            reads=[sl.b, xT.b], writes=[pb.b])
                    ctx.op(ctx.ACT, lambda: nc.scalar.copy(out=cs_.t[:], in_=pC.t[:]),
                           reads=[pC.b], writes=[cs_.b])
                    ctx.op(ctx.POOL, lambda: nc.gpsimd.tensor_copy(out=u_.t[:, 0:2], in_=car.t[:, c, :]),
                           reads=[car.b], writes=[u_.b])
                    ctx.op(ctx.DVE, lambda: nc.vector.tensor_tensor(
                        out=u_.t[:, 2:514], in0=cs_.t[:], in1=pH.t[:], op=ALU.mult),
                        reads=[cs_.b, pH.b], writes=[u_.b], acc=True)
                    ctx.op(ctx.POOL, lambda: nc.gpsimd.tensor_copy(out=car.t[:, c, :], in_=u_.t[:, 512:514]),
                           reads=[u_.b], writes=[car.b])
                    if hh == 0 and t == 0:
                        ctx.op(ctx.ACT, lambda: nc.scalar.copy(out=bfs.t[:, c, :], in_=pB.t[:, 0:2]),
                               reads=[pB.b], writes=[bfs.b], acc=True)
                    ctx.op(ctx.DVE, lambda: nc.vector.tensor_scalar(
                        out=y_.t[:], in0=u_.t[:, 2:514], scalar1=cw.t[:, c, 2:3], scalar2=None,
                        op0=ALU.mult), reads=[u_.b, cw.b], writes=[y_.b])
                    ctx.op(ctx.DVE, lambda: nc.vector.scalar_tensor_tensor(
                        out=y_.t[:], in0=u_.t[:, 1:513], scalar=cw.t[:, c, 1:2], in1=y_.t[:],
                        op0=ALU.mult, op1=ALU.add), reads=[u_.b, cw.b, y_.b], writes=[y_.b])
                    ctx.op(ctx.DVE, lambda: nc.vector.scalar_tensor_tensor(
                        out=y_.t[:], in0=u_.t[:, 0:512], scalar=cw.t[:, c, 0:1], in1=y_.t[:],
                        op0=ALU.mult, op1=ALU.add), reads=[u_.b, cw.b, y_.b], writes=[y_.b])
                    ctx.op(ctx.DVE, lambda: nc.vector.tensor_tensor(
                        out=y_.t[:], in0=y_.t[:], in1=pB.t[:], op=ALU.mult),
                        reads=[y_.b, pB.b], writes=[y_.b])
                    ctx.dma(ctx.SP, y_.s, o_cv[c * 128:(c + 1) * 128, t0 + t * 512:t0 + (t + 1) * 512],
                            y_.t[:], reads=[y_.b], writes=[OUT], acc=True)
        ctx.dma(ctx.SP, car.s, o_ul.rearrange("(c p) j -> p c j", p=128), car.t[:],
                reads=[car.b], writes=[OUT], acc=True)
        ctx.dma(ctx.SP, bfs.s, o_bf.rearrange("(c p) j -> p c j", p=128), bfs.t[:],
                reads=[bfs.b], writes=[OUT], acc=True)
        ctx.finish([OUT])


def _ident_bf16():
    return np.eye(128, dtype=np.float32).astype(ml_dtypes.bfloat16)


NQT = SEQ // 512
NKB = SEQ // 128
MASK_NEG = -30000.0
P2_PIPE = True


def phase2(nc, ctx, io):
    fused = io.get("fused", False)
    ident_d, identf_d, triu_d, tris_d = io["ident"], io["identf"], io["triu"], io["tris"]
    ones_d, mask_d, o_oT, cs_d = io["onesf"], io["maskb"], io["oT"], io["cs_scratch"]
    OUT = Buf("dram_out")
    CSD = Buf("cs_dram")

    with contextlib.ExitStack() as stack:
        A = Res(nc, ctx, stack)
        QT = A.sb("QT", [70, SEQ], BF16, dma=True)
        KT = A.sb("KT", [70, SEQ], BF16, dma=True)
        VP = A.sb("VP", [128, NKB, 65], BF16, dma=True)
        ident = A.sb("ident", [128, 128], BF16, dma=True)
        identf = A.sb("identf", [128, 128], F32, dma=True)
        triu = A.sb("triu", [128, 128], F32, dma=True)
        tris = A.sb("tris", [128, 128], F32, dma=True)
        onesf = A.sb("onesf", [128, 128], F32, dma=True)
        maskb = A.sb("maskb", [128, 4, 512], BF16, dma=True)
        fl = A.sb("fl", [128, 128], F32, dma=True)
        bfg = A.sb("bfg", [128, 1], F32, dma=True)
        one1 = A.sb("one1", [128, 1], F32)
        ex = A.sb("ex", [128, 128], F32)
        lf = A.sb("lf", [128, 128], F32)
        xts = A.sb("xts", [128, 128], F32)
        rsb = A.sb("rsb", [128, 128], F32)
        cc = A.sb("cc", [128, 128], F32)
        r1 = A.sb("r1", [128, 128], F32)
        tf = A.sb("tf", [128, 128], F32)
        spl = A.sb("spl", [128, 6, 128], BF16)
        Ps = [A.sb("P", [128, 512], BF16) for _ in range(3)]
        osb = [A.sb("osb", [65, 512], F32) for _ in range(2)]
        Sb = [A.ps("S", [128, 512], F32) for _ in range(4)]
        Ob = [A.ps("O", [128, 512], F32) for _ in range(2)]
        Mb = [A.ps("M", [128, 512], F32) for _ in range(2)]
        osem = ctx.dsem("out")
        csem = ctx.dsem("cs")

        for s_, d_ in ((ident, ident_d), (identf, identf_d), (triu, triu_d), (tris, tris_d),
                       (onesf, ones_d), (maskb, mask_d)):
            ctx.dma(ctx.SP, s_.s, s_.t[:], d_, writes=[s_.b])
        if not fused:
            ctx.dma(ctx.SP, fl.s, fl.t[:], io["fl"], writes=[fl.b])
            ctx.dma(ctx.SP, bfg.s, bfg.t[:], io["bfg"], writes=[bfg.b])
        else:
            pid = nc.sync.partition_id()
            qkv_g, fl_g, fl_loc = io["qkv_g"], io["fl_g"], io["fl_loc"]
            FLL = Buf("fl_loc")
            ctx.dma(ctx.SP, fl.s, fl_loc.rearrange("r (o t) -> r o t", o=1),
                    fl_g.rearrange("(r h) t -> r h t", h=NHEAD)[:, bass.ds(pid, 1), :], writes=[FLL])
            ctx.dma(ctx.SP, fl.s, fl.t[:], fl_loc.rearrange("r (a j) -> (r a) j", j=128),
                    reads=[FLL], writes=[fl.b])
            ctx.dma(ctx.SP, bfg.s, bfg.t[:], io["b_forget"][0:1, bass.ds(pid, 1)].broadcast_to([128, 1]),
                    writes=[bfg.b])
        ctx.op(ctx.POOL, lambda: nc.gpsimd.memset(QT.t[64:70, :], 1.0), writes=[QT.b])
        ctx.op(ctx.POOL, lambda: nc.gpsimd.memset(KT.t[64:70, :], 1.0), writes=[KT.b])
        ctx.op(ctx.POOL, lambda: nc.gpsimd.memset(one1.t[:], 1.0), writes=[one1.b])
        if not fused:
            qT_d, kT_d, v_d = io["qT"], io["kT"], io["v"]
            for q4 in range(4):
                cs = slice(q4 * 4096, (q4 + 1) * 4096)
                ctx.dma(ctx.SP, QT.s, QT.t[0:64, cs], qT_d[:, cs], writes=[QT.b], acc=True)
                ctx.dma(ctx.SP, KT.s, KT.t[0:64, cs], kT_d[:, cs], writes=[KT.b], acc=True)
            ctx.dma(ctx.SP, VP.s, VP.t[:, :, 0:64], v_d, writes=[VP.b])
        else:
            v_g = io["v_g"]
            qv = qkv_g.rearrange("(r a) t -> r a t", r=NCORES)
            ctx.dma(ctx.SP, QT.s, QT.t[0:64, :].rearrange("p (r t) -> p r t", r=NCORES),
                    qv[:, 0:512, :][:, bass.ds(pid * 64, 64), :].rearrange("r p t -> p r t"),
                    writes=[QT.b], acc=True)
            ctx.dma(ctx.SP, KT.s, KT.t[0:64, :].rearrange("p (r t) -> p r t", r=NCORES),
                    qv[:, 512:1024, :][:, bass.ds(pid * 64, 64), :].rearrange("r p t -> p r t"),
                    writes=[KT.b], acc=True)
            vv = v_g.rearrange("(r a) c -> r a c", r=NCORES)
            v_loc = io["v_loc"]
            VLOC = Buf("v_loc")
            ctx.dma(ctx.SP, VP.s, v_loc, vv[:, bass.ds(pid * 128, 128), :], writes=[VLOC])
            for r in range(NCORES):
                ctx.dma(ctx.SP, VP.s, VP.t[:, r * 16:(r + 1) * 16, 0:64],
                        v_loc[r].rearrange("p (kb d) -> p kb d", d=64),
                        reads=[VLOC], writes=[VP.b], acc=(r > 0))
        ctx.op(ctx.POOL, lambda: nc.gpsimd.memset(VP.t[:, :, 64:65], 1.0), writes=[VP.b])

        ctx.op(ctx.DVE, lambda: nc.vector.tensor_scalar(
            out=bfg.t[:], in0=bfg.t[:], scalar1=-1.0, scalar2=None, op0=ALU.mult),
            reads=[bfg.b], writes=[bfg.b])
        ctx.op(ctx.ACT, lambda: nc.scalar.activation(
            out=ex.t[:], in_=fl.t[:], func=AF.Exp, bias=bfg.t[:], scale=-1.0),
            reads=[fl.b, bfg.b], writes=[ex.b])
        ctx.op(ctx.ACT, lambda: nc.scalar.activation(
            out=ex.t[:], in_=ex.t[:], func=AF.Ln, bias=one1.t[:], scale=1.0),
            reads=[ex.b, one1.b], writes=[ex.b])
        ctx.op(ctx.DVE, lambda: nc.vector.tensor_scalar(
            out=lf.t[:], in0=ex.t[:], scalar1=-1.0, scalar2=None, op0=ALU.mult),
            reads=[ex.b], writes=[lf.b])
        m0, m1 = Mb
        ctx.mm_group([lambda: nc.tensor.transpose(out=m0.t[:, 0:128], in_=lf.t[:], identity=identf.t[:])],
                     reads=[lf.b, identf.b], writes=[m0.b])
        ctx.op(ctx.DVE, lambda: nc.vector.tensor_copy(out=xts.t[:], in_=m0.t[:, 0:128]),
               reads=[m0.b], writes=[xts.b])
        ctx.mm_group([lambda: nc.tensor.matmul(m1.t[:, 0:128], lhsT=xts.t[:], rhs=onesf.t[:],
                                               start=True, stop=True)],
                     reads=[xts.b, onesf.b], writes=[m1.b])
        ctx.op(ctx.DVE, lambda: nc.vector.tensor_copy(out=rsb.t[:], in_=m1.t[:, 0:128]),
               reads=[m1.b], writes=[rsb.b])
        ctx.mm_group([
            lambda: nc.tensor.matmul(m0.t[:, 0:128], lhsT=xts.t[:], rhs=triu.t[:], start=True, stop=False),
            lambda: nc.tensor.matmul(m0.t[:, 0:128], lhsT=tris.t[:], rhs=rsb.t[:], start=False, stop=True)],
            reads=[xts.b, triu.b, tris.b, rsb.b], writes=[m0.b])
        ctx.op(ctx.DVE, lambda: nc.vector.tensor_copy(out=cc.t[:], in_=m0.t[:, 0:128]),
               reads=[m0.b], writes=[cc.b])
        src = cc
        for i in range(3):
            ctx.op(ctx.DVE, lambda: nc.vector.tensor_copy(out=spl.t[:, i, :], in_=src.t[:]),
                   reads=[src.b], writes=[spl.b], acc=True)
            ctx.op(ctx.DVE, lambda: nc.vector.tensor_copy(out=tf.t[:], in_=spl.t[:, i, :]),
                   reads=[spl.b], writes=[tf.b])
            ctx.op(ctx.DVE, lambda: nc.vector.tensor_scalar(
                out=spl.t[:, 3 + i, :], in0=tf.t[:], scalar1=-1.0, scalar2=None, op0=ALU.mult),
                reads=[tf.b], writes=[spl.b], acc=True)
            if i < 2:
                ctx.op(ctx.DVE, lambda: nc.vector.tensor_tensor(
                    out=r1.t[:], in0=src.t[:], in1=tf.t[:], op=ALU.subtract),
                    reads=[src.b, tf.b], writes=[r1.b])
                src = r1
        for i in range(6):
            ctx.dma(ctx.SP, spl.s, cs_d[i, :].rearrange("(p j) -> p j", p=128), spl.t[:, i, :],
                    reads=[spl.b], writes=[CSD], acc=True)
        ctx.dma(ctx.SP, QT.s, QT.t[67:70, :], cs_d[0:3, :], reads=[CSD], writes=[QT.b])
        ctx.dma(ctx.SP, KT.s, KT.t[64:67, :], cs_d[3:6, :], reads=[CSD], writes=[KT.b])

        ns = 0
        for qt in range(NQT):
            qc = slice(qt * 512, (qt + 1) * 512)
            O = Ob[qt % 2]
            nkb = 4 * qt + 4

            def s_step(kb):
                S = Sb[(ns_base + kb) % 4]
                kc = slice(kb * 128, (kb + 1) * 128)
                d = kb - 4 * qt
                mms = [lambda: nc.tensor.matmul(S.t[:], lhsT=KT.t[:, kc], rhs=QT.t[:, qc],
                                                start=True, stop=(d < 0))]
                rd = [KT.b, QT.b]
                if d >= 0:
                    mms.append(lambda: nc.tensor.matmul(S.t[:], lhsT=ident.t[:], rhs=maskb.t[:, d, :],
                                                        start=False, stop=True))
                    rd += [ident.b, maskb.b]
                ctx.mm_group(mms, reads=rd, writes=[S.b])

            ns_base = ns
            if P2_PIPE:
                s_step(0)
            for kb in range(nkb):
                if P2_PIPE:
                    if kb + 1 < nkb:
                        s_step(kb + 1)
                else:
                    s_step(kb)
                S = Sb[(ns_base + kb) % 4]
                P = Ps[(ns_base + kb) % 3]
                ctx.op(ctx.ACT, lambda: nc.scalar.activation(out=P.t[:], in_=S.t[:], func=AF.Exp),
                       reads=[S.b], writes=[P.b])
                ctx.mm_group([lambda: nc.tensor.matmul(
                    O.t[0:65, :], lhsT=VP.t[:, kb, :], rhs=P.t[:],
                    start=(kb == 0), stop=(kb == nkb - 1))],
                    reads=[VP.b, P.b], writes=[O.b], acc=(kb > 0))
            ns += nkb
            ob = osb[qt % 2]
            ctx.op(ctx.DVE, lambda: nc.vector.tensor_copy(out=ob.t[:], in_=O.t[0:65, :]),
                   reads=[O.b], writes=[ob.b])
            ctx.dma(ctx.SP, ob.s, o_oT[:, qc], ob.t[:], reads=[ob.b], writes=[OUT], acc=True)
        ctx.finish([OUT])


def _phase2_consts():
    j = np.arange(128)
    triu = (j[:, None] <= j[None, :]).astype(np.float32)
    tris = (j[:, None] < j[None, :]).astype(np.float32)
    k = np.arange(128)[:, None, None]
    d = np.arange(4)[None, :, None]
    q = np.arange(512)[None, None, :]
    mask = np.where(128 * d + k > q, MASK_NEG, 0.0).astype(np.float32).astype(ml_dtypes.bfloat16)
    return dict(ident=_ident_bf16(), identf=np.eye(128, dtype=np.float32), triu=triu, tris=tris,
                onesf=np.ones((128, 128), np.float32), maskb=mask)


def phase3(nc, ctx, io):
    fused = io.get("fused", False)
    x1_d, cv_d, bf_d, cw_d = io["x1"], io["convT"], io["bfirst"], io["conv_wT"]
    ga_d, gc_d, wmo_d = io["g_attn"], io["g_conv"], io["w_mo"]
    ln_d = [(io[f"ln{i}_g"], io[f"ln{i}_b"]) for i in (2, 3, 4)]
    w_in, w_out, p_d, wp_d, wg_d, bg_d = io["w_in"], io["w_out"], io["p"], io["w_ple"], io["w_pg"], io["b_pg"]
    ident_d, o_out, xs2_d, xs3_d = io["ident"], io["out"], io["xs2"], io["xs3"]
    if fused:
        pid = nc.sync.partition_id()
        o_g, o_loc = io["o_g"], io["o_loc"]
        up_d = io["ul_g"][bass.ds(((pid + (NCORES - 1)) % NCORES) * 512, 512), :]
        oT_rows = lambda head, r0, r1, t0: o_loc[head * 65 + r0:head * 65 + r1, t0:t0 + HALF]
    else:
        up_d = io["uprev"]
        oT_d = io["oT"]
        oT_rows = lambda head, r0, r1, t0: oT_d[head, r0:r1, t0:t0 + HALF]
    OUT, XS2, XS3, OLOC = Buf("dram_out"), Buf("xs2"), Buf("xs3"), Buf("o_loc")

    with contextlib.ExitStack() as stack:
        A = Res(nc, ctx, stack)
        R = {}
        ident = A.sb("ident", [128, 128], BF16, dma=True)
        if fused:
            ctx.dma(ctx.SP, ctx.dsem("oloc"), o_loc, o_g[:, bass.ds(pid * T, T)], writes=[OLOC])
        R["wslots"] = [A.sb("wsl", [128, 8, 256], BF16, dma=True) for _ in range(3)]
        Wres = A.sb("Wres", [128, NJ, D], BF16, dma=True)
        R["Wout"] = Wres
        Wg = A.sb("Wg", [128, 8, D], BF16, dma=True)
        Wp = A.sb("Wp", [128, 2, D], BF16, dma=True)
        GT = A.sb("GT", [128, NJ, HALF], BF16)
        xT = A.sb("xT", [128, 8, HALF], BF16)
        pT = A.sb("pT", [128, 2, HALF], BF16)
        Fs = [A.sb("F", [128, D], F32, dma=True) for _ in range(6)]
        gbs = [A.sb("gb", [128, 2, D], F32, dma=True) for _ in range(2)]
        bg = A.sb("bg", [128, 1, D], F32, dma=True)
        xb = [A.sb("xb", [128, D], BF16) for _ in range(2)]
        pf = [A.sb("pf", [128, PLE], F32, dma=True) for _ in range(2)]
        pb = [A.sb("pb", [128, PLE], BF16) for _ in range(2)]
        R["sg"] = [A.sb("sg", [128, 512], F32) for _ in range(2)]
        R["ln_st"] = A.sb("lnst", [128, 2, 6], F32)
        R["ln_mv"] = A.sb("lnmv", [128, 2], F32)
        R["ln_rs"] = A.sb("lnrs", [128, 1], F32)
        R["ln_nm"] = A.sb("lnnm", [128, 1], F32)
        R["eps_ln"] = A.sb("epsln", [128, 1], F32)
        eps_rms = A.sb("epsrms", [128, 1], F32)
        onesb = A.sb("onesb", [128, 1], BF16)
        rrs = [A.sb("rr", [128, 2], F32) for _ in range(2)]
        ga = A.sb("ga", [128, 4], F32, dma=True)
        gc = A.sb("gc", [128, 4], F32, dma=True)
        up = A.sb("up", [128, 4, 2], F32, dma=True)
        bfr = A.sb("bfr", [128, 4, 2], F32, dma=True)
        cw = A.sb("cw", [128, 4, 3], F32, dma=True)
        fx = A.sb("fx", [128, 4, 2], F32)
        ft = A.sb("ft", [128, 4, 2], F32)
        banks = [A.ps("bk", [128, 512], F32) for _ in range(6)]
        tps = [A.ps("tp", [128, 1024], BF16) for _ in range(2)]
        R["G"] = [banks[0], banks[1]]
        R["U"] = [banks[2], banks[3]]
        R["Y"] = [(banks[0], banks[2]), (banks[1], banks[3])]
        osem = ctx.dsem("out")
        s2sem = ctx.dsem("xs2")
        s3sem = ctx.dsem("xs3")
        wgv = wg_d.rearrange("(k p) f -> p k f", p=128)
        wpv = wp_d.rearrange("(k p) f -> p k f", p=128)
        wmov = wmo_d.rearrange("(k p) f -> p k f", p=128)
        cview = lambda a: a.rearrange("(c p) j -> p c j", p=128)
        mergedT = lambda c, cols: GT.t[:, c, cols]
        sqv = lambda c, cols: GT.t[:, 8 + c, cols]
        state = {"f": 0, "xb": 0, "tp": 0, "gb": 0}

        def nextF():
            s = Fs[state["f"] % len(Fs)]
            state["f"] += 1
            return s

        def load_gb(i):
            s = gbs[state["gb"] % 2]
            state["gb"] += 1
            ctx.dma(ctx.SP, s.s, s.t[:, 0, :], ln_d[i][0].partition_broadcast(128), writes=[s.b])
            ctx.dma(ctx.SP, s.s, s.t[:, 1, :], ln_d[i][1].partition_broadcast(128), writes=[s.b], acc=True)
            return s

        def to_xT(src, nch, dstT, cols, src_is_f32=True):
            xbb = xb[state["xb"] % 2]
            state["xb"] += 1
            ctx.op(ctx.ACT, lambda: nc.scalar.copy(out=xbb.t[:, 0:nch * 128], in_=src.t[:, 0:nch * 128]),
                   reads=[src.b], writes=[xbb.b])
            emit_transposes(ctx, xbb, nch, ident, tps[state["tp"] % 2], dstT, cols, ctx.DVE)
            state["tp"] += 1

        ctx.dma(ctx.SP, ident.s, ident.t[:], ident_d, writes=[ident.b])
        ctx.dma(ctx.SP, bg.s, bg.t[:, 0, :], bg_d.partition_broadcast(128), writes=[bg.b])
        for s_, d_ in ((up, up_d), (bfr, bf_d), (cw, cw_d)):
            ctx.dma(ctx.SP, s_.s, s_.t[:], cview(d_), writes=[s_.b])
        for s_, d_ in ((ga, ga_d), (gc, gc_d)):
            ctx.dma(ctx.SP, s_.s, s_.t[:], d_, writes=[s_.b])
        ctx.op(ctx.POOL, lambda: nc.gpsimd.memset(R["eps_ln"].t[:], float(EPS_LN)), writes=[R["eps_ln"].b])
        ctx.op(ctx.POOL, lambda: nc.gpsimd.memset(eps_rms.t[:], float(RMS_EPS)), writes=[eps_rms.b])
        ctx.op(ctx.POOL, lambda: nc.gpsimd.memset(onesb.t[:], 1.0), writes=[onesb.b])
        ctx.dma(ctx.POOL, Wg.s, Wg.t[:], wgv, writes=[Wg.b])
        ctx.dma(ctx.POOL, Wp.s, Wp.t[:], wpv, writes=[Wp.b])
        V = ctx.DVE
        if fused:
            hm = A.sb("hm", [128, 1], F32)
            ctx.dma(ctx.SP, hm.s, hm.t[:], io["halo_mask"], writes=[hm.b])
            ctx.op(V, lambda: nc.vector.tensor_scalar(
                out=up.t[:], in0=up.t[:], scalar1=hm.t[:, 0:1], scalar2=None, op0=ALU.mult),
                reads=[up.b, hm.b], writes=[up.b])
        ctx.op(V, lambda: nc.vector.tensor_tensor(out=ft.t[:, :, 0:1], in0=up.t[:, :, 0:1], in1=cw.t[:, :, 0:1], op=ALU.mult),
               reads=[up.b, cw.b], writes=[ft.b])
        ctx.op(V, lambda: nc.vector.tensor_tensor(out=ft.t[:, :, 1:2], in0=up.t[:, :, 1:2], in1=cw.t[:, :, 1:2], op=ALU.mult),
               reads=[up.b, cw.b], writes=[ft.b])
        ctx.op(V, lambda: nc.vector.tensor_tensor(out=ft.t[:, :, 0:1], in0=ft.t[:, :, 0:1], in1=ft.t[:, :, 1:2], op=ALU.add),
               reads=[ft.b], writes=[ft.b])
        ctx.op(V, lambda: nc.vector.tensor_tensor(out=fx.t[:, :, 0:1], in0=ft.t[:, :, 0:1], in1=bfr.t[:, :, 0:1], op=ALU.mult),
               reads=[ft.b, bfr.b], writes=[fx.b])
        ctx.op(V, lambda: nc.vector.tensor_tensor(out=ft.t[:, :, 1:2], in0=up.t[:, :, 1:2], in1=cw.t[:, :, 0:1], op=ALU.mult),
               reads=[up.b, cw.b, fx.b], writes=[ft.b])
        ctx.op(V, lambda: nc.vector.tensor_tensor(out=fx.t[:, :, 1:2], in0=ft.t[:, :, 1:2], in1=bfr.t[:, :, 1:2], op=ALU.mult),
               reads=[ft.b, bfr.b], writes=[fx.b])

        for hh in range(NHALF):
            t0 = hh * HALF
            hc = slice(t0, t0 + HALF)
            full = slice(0, HALF)
            ctx.dma(ctx.POOL, Wres.s, Wres.t[:, 0:8, :], wmov, writes=[Wres.b])
            for c in range(4):
                Ab, Lb = nextF(), nextF()
                for hd in range(2):
                    ps_ = slice(hd * 64, (hd + 1) * 64)
                    ctx.dma(ctx.SP, Ab.s, Ab.t[ps_, :], oT_rows(2 * c + hd, 0, 64, t0), reads=[OLOC], writes=[Ab.b], acc=(hd > 0))
                    ctx.dma(ctx.SP, Lb.s, Lb.t[ps_, :], oT_rows(2 * c + hd, 64, 65, t0).broadcast_to([64, HALF]),
                            reads=[OLOC], writes=[Lb.b], acc=(hd > 0))
                ctx.op(ctx.DVE, lambda: nc.vector.reciprocal(out=Lb.t[:], in_=Lb.t[:]), reads=[Lb.b], writes=[Lb.b])
                ctx.op(ctx.DVE, lambda: nc.vector.tensor_tensor(out=Ab.t[:], in0=Ab.t[:], in1=Lb.t[:], op=ALU.mult),
                       reads=[Ab.b, Lb.b], writes=[Ab.b])
                ctx.op(ctx.ACT, lambda: nc.scalar.activation(out=sqv(c, full), in_=Ab.t[:], func=AF.Square),
                       reads=[Ab.b], writes=[GT.b], acc=True)
                ctx.op(ctx.ACT, lambda: nc.scalar.activation(
                    out=mergedT(c, full), in_=Ab.t[:], func=AF.Copy, scale=ga.t[:, c:c + 1]),
                    reads=[Ab.b, ga.b], writes=[GT.b], acc=True)
            for c in range(4):
                Cb = nextF()
                ctx.dma(ctx.SP, Cb.s, Cb.t[:], cv_d[c * 128:(c + 1) * 128, hc], writes=[Cb.b])
                if hh == 0:
                    ctx.op(ctx.DVE, lambda: nc.vector.tensor_tensor(
                        out=Cb.t[:, 0:2], in0=Cb.t[:, 0:2], in1=fx.t[:, c, :], op=ALU.add),
                        reads=[Cb.b, fx.b], writes=[Cb.b])
                ctx.op(ctx.ACT, lambda: nc.scalar.activation(out=sqv(4 + c, full), in_=Cb.t[:], func=AF.Square),
                       reads=[Cb.b], writes=[GT.b], acc=True)
                ctx.op(ctx.ACT, lambda: nc.scalar.activation(
                    out=mergedT(4 + c, full), in_=Cb.t[:], func=AF.Copy, scale=gc.t[:, c:c + 1]),
                    reads=[Cb.b, gc.b], writes=[GT.b], acc=True)
            gb = load_gb(0)

            def c_mm(b):
                tok = slice(b * 128, (b + 1) * 128)
                ss = banks[4]
                mms = []
                for grp in range(2):
                    for c in range(4):
                        mms.append(lambda grp=grp, c=c: nc.tensor.matmul(
                            ss.t[:, grp:grp + 1], lhsT=sqv(4 * grp + c, tok), rhs=onesb.t[:],
                            start=(c == 0), stop=(c == 3)))
                ctx.mm_group(mms, reads=[GT.b, onesb.b], writes=[ss.b])
                for n in range(2):
                    for grp in range(2):
                        pbk = banks[grp * 2 + n]
                        ctx.mm_group([lambda c=c, pbk=pbk: nc.tensor.matmul(
                            pbk.t[:], lhsT=mergedT(4 * grp + c, tok), rhs=Wres.t[:, 4 * grp + c, n * 512:(n + 1) * 512],
                            start=(c == 0), stop=(c == 3)) for c in range(4)],
                            reads=[GT.b, Wres.b], writes=[pbk.b])

            def c_evac(b):
                rows = slice(t0 + b * 128, t0 + (b + 1) * 128)
                ss = banks[4]
                rr = rrs[b % 2]
                ctx.op(ctx.ACT, lambda: nc.scalar.activation(
                    out=rr.t[:], in_=ss.t[:, 0:2], func=AF.Sqrt, bias=eps_rms.t[:], scale=1.0 / 512.0),
                    reads=[ss.b, eps_rms.b], writes=[rr.b])
                ctx.op(ctx.DVE, lambda: nc.vector.reciprocal(out=rr.t[:], in_=rr.t[:]), reads=[rr.b], writes=[rr.b])
                xr, tmp, z = nextF(), nextF(), nextF()
                ctx.dma(ctx.SP, xr.s, xr.t[:], x1_d[rows, :], writes=[xr.b])
                for n in range(2):
                    cs = slice(n * 512, (n + 1) * 512)
                    pa, pc = banks[n], banks[2 + n]
                    ctx.op(ctx.ACT, lambda: nc.scalar.activation(
                        out=tmp.t[:, cs], in_=pa.t[:], func=AF.Copy, scale=rr.t[:, 0:1]),
                        reads=[pa.b, rr.b], writes=[tmp.b], acc=(n > 0))
                    ctx.op(ctx.DVE, lambda: nc.vector.scalar_tensor_tensor(
                        out=tmp.t[:, cs], in0=pc.t[:], scalar=rr.t[:, 1:2], in1=tmp.t[:, cs],
                        op0=ALU.mult, op1=ALU.add), reads=[pc.b, rr.b, tmp.b], writes=[tmp.b], acc=True)
                    ctx.op(ctx.DVE, lambda: nc.vector.scalar_tensor_tensor(
                        out=z.t[:, cs], in0=tmp.t[:, cs], scalar=float(1.0 / ALPHA), in1=xr.t[:, cs],
                        op0=ALU.mult, op1=ALU.add), reads=[tmp.b, xr.b], writes=[z.b], acc=(n > 0))
                return z

            def c_chain(b, z):
                tok = slice(b * 128, (b + 1) * 128)
                rows = slice(t0 + b * 128, t0 + (b + 1) * 128)
                x2 = nextF()
                emit_layernorm(ctx, R, z, gb, x2, "eps_ln")
                ctx.dma(ctx.SP, x2.s, xs2_d[rows, :], x2.t[:], reads=[x2.b], writes=[XS2], acc=True)
                to_xT(x2, 8, xT, tok)

            c_mm(0)
            for b in range(NBLK):
                z = c_evac(b)
                if b + 1 < NBLK:
                    c_mm(b + 1)
                c_chain(b, z)
            emit_load_wout(ctx, R, w_out)
            emit_ffn_h(ctx, R, w_in, xT, GT)
            gb = load_gb(1)
            emit_ffn_y_mm(ctx, R, GT, 0)
            for b in range(NBLK):
                tok = slice(b * 128, (b + 1) * 128)
                rows = slice(t0 + b * 128, t0 + (b + 1) * 128)
                xr, z, x3 = nextF(), nextF(), nextF()
                ctx.dma(ctx.SP, xr.s, xr.t[:], xs2_d[rows, :], reads=[XS2], writes=[xr.b])
                if b + 1 < NBLK:
                    emit_ffn_y_mm(ctx, R, GT, b + 1)
                emit_ffn_y_combine(ctx, R, b, xr, z, 0.5 / ALPHA)
                emit_layernorm(ctx, R, z, gb, x3, "eps_ln")
                ctx.dma(ctx.SP, x3.s, xs3_d[rows, :], x3.t[:], reads=[x3.b], writes=[XS3], acc=True)
                to_xT(x3, 8, xT, tok)
            gb = load_gb(2)

            def e_mm(b):
                tok = slice(b * 128, (b + 1) * 128)
                rows = slice(t0 + b * 128, t0 + (b + 1) * 128)
                pfs = pf[b % 2]
                ctx.dma(ctx.SP, pfs.s, pfs.t[:], p_d[rows, :], writes=[pfs.b])
                to_xT(pfs, 2, pT, tok)
                for n in range(2):
                    gbk, pbk = banks[n], banks[2 + n]
                    ctx.mm_group([lambda k=k, gbk=gbk: nc.tensor.matmul(
                        gbk.t[:], lhsT=xT.t[:, k, tok], rhs=Wg.t[:, k, n * 512:(n + 1) * 512],
                        start=(k == 0), stop=(k == 7)) for k in range(8)],
                        reads=[xT.b, Wg.b], writes=[gbk.b])
                    ctx.mm_group([lambda k=k, pbk=pbk: nc.tensor.matmul(
                        pbk.t[:], lhsT=pT.t[:, k, tok], rhs=Wp.t[:, k, n * 512:(n + 1) * 512],
                        start=(k == 0), stop=(k == 1)) for k in range(2)],
                        reads=[pT.b, Wp.b], writes=[pbk.b])

            def e_evac(b):
                rows = slice(t0 + b * 128, t0 + (b + 1) * 128)
                xr, tmp, z = nextF(), nextF(), nextF()
                ctx.dma(ctx.SP, xr.s, xr.t[:], xs3_d[rows, :], reads=[XS3], writes=[xr.b])
                for n in range(2):
                    cs = slice(n * 512, (n + 1) * 512)
                    gbk, pbk = banks[n], banks[2 + n]
                    ctx.op(ctx.DVE, lambda: nc.vector.tensor_tensor(
                        out=tmp.t[:, cs], in0=gbk.t[:], in1=bg.t[:, 0, cs], op=ALU.add),
                        reads=[gbk.b, bg.b], writes=[tmp.b], acc=(n > 0))
                    ctx.op(ctx.ACT, lambda: nc.scalar.activation(
                        out=tmp.t[:, cs], in_=tmp.t[:, cs], func=AF.Sigmoid),
                        reads=[tmp.b], writes=[tmp.b], acc=True)
                    ctx.op(ctx.DVE, lambda: nc.vector.tensor_tensor(
                        out=tmp.t[:, cs], in0=tmp.t[:, cs], in1=pbk.t[:], op=ALU.mult),
                        reads=[tmp.b, pbk.b], writes=[tmp.b], acc=True)
                    ctx.op(ctx.DVE, lambda: nc.vector.scalar_tensor_tensor(
                        out=z.t[:, cs], in0=tmp.t[:, cs], scalar=float(1.0 / ALPHA), in1=xr.t[:, cs],
                        op0=ALU.mult, op1=ALU.add), reads=[tmp.b, xr.b], writes=[z.b], acc=(n > 0))
                return z

            def e_chain(b, z):
                rows = slice(t0 + b * 128, t0 + (b + 1) * 128)
                xo = nextF()
                emit_layernorm(ctx, R, z, gb, xo, "eps_ln")
                ctx.dma(ctx.SP, xo.s, o_out[rows, :], xo.t[:], reads=[xo.b], writes=[OUT], acc=True)

            e_mm(0)
            for b in range(NBLK):
                z = e_evac(b)
                if b + 1 < NBLK:
                    e_mm(b + 1)
                e_chain(b, z)
        ctx.finish([OUT])


def _all_gather(nc, sem, count, src, dst):
    nc.gpsimd.collective_compute("AllGather", ALU.bypass, replica_groups=[list(range(NCORES))],
                                 ins=[src.opt()], outs=[dst.opt()]).then_inc(sem, 1)
    nc.gpsimd.wait_ge(sem, count)


def _core_barrier(nc):
    nc.all_engine_barrier()
    nc.all_core_barrier()


def build_fused():
    nc = bass.Bass("TRN2", target_bir_lowering=False, num_devices=NCORES)
    din = lambda n, s, dt=F32: nc.dram_tensor(n, list(s), dt, kind="ExternalInput").ap()
    loc = lambda n, s, dt=F32: nc.dram_tensor(n, list(s), dt).ap()
    shr = lambda n, s, dt=F32: nc.dram_tensor(n, list(s), dt, addr_space="Shared").ap()
    I = dict(
        x=din("x", [T, D]), p=din("p", [T, PLE]),
        w1_in=din("w1_in", [D, 2 * DFF]), w1_out=din("w1_out", [DFF, D]),
        ln1_g=din("ln1_g", [1, D]), ln1_b=din("ln1_b", [1, D]),
        w_mix=din("w_mix", [D, DPROJ]), b_forget=din("b_forget", [1, NHEAD]),
        conv_wT=din("conv_wT", [512, 3]), g_attn=din("g_attn", [128, 4]), g_conv=din("g_conv", [128, 4]),
        w_mo=din("w_mo", [D, D]), ln2_g=din("ln2_g", [1, D]), ln2_b=din("ln2_b", [1, D]),
        w2_in=din("w2_in", [D, 2 * DFF]), w2_out=din("w2_out", [DFF, D]),
        ln3_g=din("ln3_g", [1, D]), ln3_b=din("ln3_b", [1, D]),
        w_ple=din("w_ple", [PLE, D]), w_pg=din("w_pg", [D, D]), b_pg=din("b_pg", [1, D]),
        ln4_g=din("ln4_g", [1, D]), ln4_b=din("ln4_b", [1, D]),
        ident=din("ident", [128, 128], BF16), identf=din("identf", [128, 128]),
        triu=din("triu", [128, 128]), tris=din("tris", [128, 128]), onesf=din("onesf", [128, 128]),
        maskb=din("maskb", [128, 4, 512], BF16), halo_mask=din("halo_mask", [128, 1]),
    )
    out = nc.dram_tensor("out", [T, D], F32, kind="ExternalOutput").ap()
    x1s, cvs, bfs_d = loc("x1_spill", [T, D]), loc("cv_spill", [512, T]), loc("bf_spill", [512, 2])
    xs2, xs3 = loc("xs2", [T, D]), loc("xs3", [T, D])
    qkv_in, fl_in, ul_in = loc("qkv_in", [1024, T], BF16), loc("fl_in", [NHEAD, T]), loc("ul_in", [512, 2])
    v_in = loc("v_in", [NHEAD * 128, 1024], BF16)
    qkv_g = shr("qkv_g", [NCORES * 1024, T], BF16)
    v_g = shr("v_g", [NCORES * NHEAD * 128, 1024], BF16)
    fl_g = shr("fl_g", [NCORES * NHEAD, T])
    ul_g = shr("ul_g", [NCORES * 512, 2])
    o_in = loc("o_in", [65, SEQ])
    o_g = shr("o_g", [NCORES * 65, SEQ])
    cs_scr = loc("cs_scratch", [6, SEQ], BF16)
    fl_loc = loc("fl_loc", [NCORES, T])
    v_loc = loc("v_loc", [NCORES, 128, 1024], BF16)
    o_loc = loc("o_loc", [NCORES * 65, T])
    ccsem = nc.alloc_semaphore("cc_sem")

    ctx = Ctx(nc)
    _core_barrier(nc)
    phase1(nc, ctx, dict(
        x=I["x"], w_in=I["w1_in"], w_out=I["w1_out"], ln_g=I["ln1_g"], ln_b=I["ln1_b"], w_mix=I["w_mix"],
        conv_wT=I["conv_wT"], ident=I["ident"], x1=x1s, qT=qkv_in[0:512, :], kT=qkv_in[512:1024, :],
        v_dst=lambda kb: v_in.rearrange("(h p) (kb d) -> p h kb d", p=128, d=64)[:, :, kb, :], fl=fl_in, convT=cvs,
        ulast=ul_in, bfirst=bfs_d))
    _core_barrier(nc)
    _all_gather(nc, ccsem, 1, qkv_in, qkv_g)
    _all_gather(nc, ccsem, 2, v_in, v_g)
    _all_gather(nc, ccsem, 3, fl_in, fl_g)
    _all_gather(nc, ccsem, 4, ul_in, ul_g)
    _core_barrier(nc)
    phase2(nc, ctx, dict(
        fused=True, ident=I["ident"], identf=I["identf"], triu=I["triu"], tris=I["tris"], onesf=I["onesf"],
        maskb=I["maskb"], oT=o_in, cs_scratch=cs_scr, qkv_g=qkv_g, v_g=v_g, fl_g=fl_g, fl_loc=fl_loc, v_loc=v_loc, b_forget=I["b_forget"]))
    _core_barrier(nc)
    _all_gather(nc, ccsem, 5, o_in, o_g)
    _core_barrier(nc)
    phase3(nc, ctx, dict(
        fused=True, x1=x1s, convT=cvs, bfirst=bfs_d, conv_wT=I["conv_wT"], g_attn=I["g_attn"], g_conv=I["g_conv"],
        w_mo=I["w_mo"], ln2_g=I["ln2_g"], ln2_b=I["ln2_b"], ln3_g=I["ln3_g"], ln3_b=I["ln3_b"],
        ln4_g=I["ln4_g"], ln4_b=I["ln4_b"], w_in=I["w2_in"], w_out=I["w2_out"], p=I["p"], w_ple=I["w_ple"],
        w_pg=I["w_pg"], b_pg=I["b_pg"], ident=I["ident"], out=out, xs2=xs2, xs3=xs3, o_g=o_g, o_loc=o_loc, ul_g=ul_g,
        halo_mask=I["halo_mask"]))
    return nc


def _std(nc):
    din = lambda n, s, dt=F32: nc.dram_tensor(n, list(s), dt, kind="ExternalInput").ap()
    dout = lambda n, s, dt=F32: nc.dram_tensor(n, list(s), dt, kind="ExternalOutput").ap()
    return din, dout


def build_phase1():
    nc = bass.Bass("TRN2", target_bir_lowering=False)
    din, dout = _std(nc)
    v = dout("v", [T, 512], BF16)
    io = dict(x=din("x", [T, D]), w_in=din("w_in", [D, 2 * DFF]), w_out=din("w_out", [DFF, D]),
              ln_g=din("ln_g", [1, D]), ln_b=din("ln_b", [1, D]), w_mix=din("w_mix", [D, DPROJ]),
              conv_wT=din("conv_wT", [512, 3]), ident=din("ident", [128, 128], BF16),
              x1=dout("x1", [T, D]), qT=dout("qT", [512, T], BF16), kT=dout("kT", [512, T], BF16),
              v_dst=lambda kb: v[kb * 128:(kb + 1) * 128, :].rearrange("p (h d) -> p h d", d=64),
              fl=dout("fl", [8, T]), convT=dout("convT", [512, T]), ulast=dout("ulast", [512, 2]),
              bfirst=dout("bfirst", [512, 2]))
    phase1(nc, Ctx(nc), io)
    return nc


def build_phase2():
    nc = bass.Bass("TRN2", target_bir_lowering=False)
    din, dout = _std(nc)
    io = dict(qT=din("qT", [64, SEQ], BF16), kT=din("kT", [64, SEQ], BF16), v=din("v", [128, NKB, 64], BF16),
              fl=din("fl", [128, 128]), bfg=din("bfg", [128, 1]), ident=din("ident", [128, 128], BF16),
              identf=din("identf", [128, 128]), triu=din("triu", [128, 128]), tris=din("tris", [128, 128]),
              onesf=din("onesf", [128, 128]), maskb=din("maskb", [128, 4, 512], BF16),
              oT=dout("oT", [65, SEQ]), cs_scratch=nc.dram_tensor("cs_scratch", [6, SEQ], BF16).ap())
    phase2(nc, Ctx(nc), io)
    return nc


def build_phase3():
    nc = bass.Bass("TRN2", target_bir_lowering=False)
    din, dout = _std(nc)
    io = dict(x1=din("x1", [T, D]), oT=din("oT", [NHEAD, 65, T]), convT=din("convT", [512, T]),
              uprev=din("uprev", [512, 2]), bfirst=din("bfirst", [512, 2]), conv_wT=din("conv_wT", [512, 3]),
              g_attn=din("g_attn", [128, 4]), g_conv=din("g_conv", [128, 4]), w_mo=din("w_mo", [D, D]),
              w_in=din("w_in", [D, 2 * DFF]), w_out=din("w_out", [DFF, D]), p=din("p", [T, PLE]),
              w_ple=din("w_ple", [PLE, D]), w_pg=din("w_pg", [D, D]), b_pg=din("b_pg", [1, D]),
              ident=din("ident", [128, 128], BF16), out=dout("out", [T, D]),
              xs2=nc.dram_tensor("xs2", [T, D], F32).ap(), xs3=nc.dram_tensor("xs3", [T, D], F32).ap())
    for i in (2, 3, 4):
        io[f"ln{i}_g"] = din(f"ln{i}_g", [1, D])
        io[f"ln{i}_b"] = din(f"ln{i}_b", [1, D])
    phase3(nc, Ctx(nc), io)
    return nc


def _kernel_unfused(I, x2d, p2d):
    cores = list(range(NCORES))
    ident = I["ident"]
    if "p1" not in _CACHE:
        _CACHE["p1"], _CACHE["p2"], _CACHE["p3"] = build_phase1(), build_phase2(), build_phase3()
    maps = [dict(x=np.ascontiguousarray(x2d[c * T:(c + 1) * T]), w_in=I["w1_in"], w_out=I["w1_out"],
                 ln_g=I["ln1_g"], ln_b=I["ln1_b"], w_mix=I["w_mix"], conv_wT=I["conv_wT"], ident=ident)
            for c in cores]
    r1 = run_bass_kernel_spmd(_CACHE["p1"], maps, core_ids=cores).results
    qT = np.concatenate([r1[c]["qT"] for c in cores], axis=1)
    kT = np.concatenate([r1[c]["kT"] for c in cores], axis=1)
    v = np.concatenate([r1[c]["v"] for c in cores], axis=0)
    fl = np.concatenate([r1[c]["fl"] for c in cores], axis=1)
    consts = {k: I[k] for k in ("ident", "identf", "triu", "tris", "onesf", "maskb")}
    maps = []
    for h in cores:
        vh = v[:, h * 64:(h + 1) * 64].reshape(NKB, 128, 64).transpose(1, 0, 2)
        maps.append(dict(qT=np.ascontiguousarray(qT[h * 64:(h + 1) * 64]), kT=np.ascontiguousarray(kT[h * 64:(h + 1) * 64]),
                         v=np.ascontiguousarray(vh), fl=np.ascontiguousarray(fl[h].reshape(128, 128)),
                         bfg=np.full((128, 1), I["b_forget"][0, h], np.float32), **consts))
    r2 = run_bass_kernel_spmd(_CACHE["p2"], maps, core_ids=cores).results
    oT = np.stack([r2[h]["oT"] for h in cores], axis=0)
    maps = []
    for c in cores:
        uprev = r1[c - 1]["ulast"] if c > 0 else np.zeros((512, 2), np.float32)
        m = dict(x1=r1[c]["x1"], oT=np.ascontiguousarray(oT[:, :, c * T:(c + 1) * T]), convT=r1[c]["convT"],
                 uprev=np.ascontiguousarray(uprev), bfirst=r1[c]["bfirst"], conv_wT=I["conv_wT"],
                 g_attn=I["g_attn"], g_conv=I["g_conv"], w_mo=I["w_mo"], w_in=I["w2_in"], w_out=I["w2_out"],
                 p=np.ascontiguousarray(p2d[c * T:(c + 1) * T]), w_ple=I["w_ple"], w_pg=I["w_pg"], b_pg=I["b_pg"],
                 ident=ident)
        for i in (2, 3, 4):
            m[f"ln{i}_g"], m[f"ln{i}_b"] = I[f"ln{i}_g"], I[f"ln{i}_b"]
        maps.append(m)
    r3 = run_bass_kernel_spmd(_CACHE["p3"], maps, core_ids=cores).results
    if _DBG is not None:
        _DBG.update(r1=r1, r2=r2, r3=r3)
    return np.concatenate([r3[c]["out"] for c in cores], axis=0)


_CACHE = {}
_DBG = None
FUSED = False


def kernel(x, p, ffn1_w_in, ffn1_w_out, ln1_g, ln1_b, w_mix_in, b_forget, conv_w,
           g_attn, g_conv, w_mix_out, ln2_g, ln2_b, ffn2_w_in, ffn2_w_out, ln3_g, ln3_b,
           w_ple, w_ple_gate, b_ple_gate, ln4_g, ln4_b):
    f32 = lambda a: np.ascontiguousarray(np.asarray(a, dtype=np.float32))
    x2d = f32(x)[0]
    p2d = f32(p)[0, 0]
    cores = list(range(NCORES))
    consts = _phase2_consts()
    shared = dict(
        w1_in=f32(ffn1_w_in)[0], w1_out=f32(ffn1_w_out)[0], ln1_g=f32(ln1_g), ln1_b=f32(ln1_b),
        w_mix=f32(w_mix_in)[0], b_forget=f32(b_forget).reshape(1, NHEAD),
        conv_wT=np.ascontiguousarray(f32(conv_w)[0].T),
        g_attn=np.ascontiguousarray(f32(g_attn).reshape(4, 128).T),
        g_conv=np.ascontiguousarray(f32(g_conv).reshape(4, 128).T),
        w_mo=f32(w_mix_out)[0], ln2_g=f32(ln2_g), ln2_b=f32(ln2_b),
        w2_in=f32(ffn2_w_in)[0], w2_out=f32(ffn2_w_out)[0], ln3_g=f32(ln3_g), ln3_b=f32(ln3_b),
        w_ple=f32(w_ple)[0], w_pg=f32(w_ple_gate)[0], b_pg=f32(b_ple_gate),
        ln4_g=f32(ln4_g), ln4_b=f32(ln4_b), **consts)
    if not FUSED:
        out = _kernel_unfused(shared, x2d, p2d)
        return out.reshape(1, SEQ, D).astype(np.float32)
    maps = []
    for c in cores:
        m = dict(shared)
        m["x"] = np.ascontiguousarray(x2d[c * T:(c + 1) * T])
        m["p"] = np.ascontiguousarray(p2d[c * T:(c + 1) * T])
        m["halo_mask"] = np.full((128, 1), 0.0 if c == 0 else 1.0, np.float32)
        maps.append(m)
    if "fused" not in _CACHE:
        _CACHE["fused"] = build_fused()
    res = run_bass_kernel_spmd(_CACHE["fused"], maps, core_ids=cores).results
    out = np.concatenate([res[c]["out"] for c in cores], axis=0)
    return out.reshape(1, SEQ, D).astype(np.float32)
```
